# Optimizing a Trainium2 kernel written in Bass

```python
import math
import jax, jax.numpy as jnp
from jax import lax
import numpy as np

D_MODEL = 4096
BATCH = 2
SEQ = 4096
DEPTH = 2

D_FF = 11008
W_A = D_MODEL // 2
A_GROUPS = 8
CHUNK = 128
W_B = D_MODEL // 2
CONV_WIDTH = 31
W_C = D_MODEL // 2
POOL_WINDOWS = (2, 4, 8, 16)
POOL_GROUP = W_C // len(POOL_WINDOWS)
D_HEADS = 16
D_HEAD_DIM = 128
W_D = D_HEADS * D_HEAD_DIM
Q_RANK = D_MODEL // 4
KV_RANK = D_MODEL // 8
IDX_HEADS = 32
IDX_DIM = 64
TOPK_MAX = 256
QUERY_BLOCK = 128
REL_BUCKETS = 32
REL_MAX_DIST = 128

N_EVEN = (DEPTH + 1) // 2
N_ODD = DEPTH // 2
AB_IN = 2 * W_A + 2 * W_B
CD_IN = W_C + Q_RANK + KV_RANK + IDX_DIM + IDX_HEADS
EPS = 1e-6

kernel_name = 'hybrid_gmlp_conv_pool_dsa_macaron'


def rms_norm(x, g):
    xf = x.astype(jnp.float32)
    y = xf * lax.rsqrt(jnp.mean(xf * xf, axis=-1, keepdims=True) + EPS)
    return (y * g.astype(jnp.float32)).astype(x.dtype)


def layer_norm(x, g, b):
    xf = x.astype(jnp.float32)
    mu = jnp.mean(xf, axis=-1, keepdims=True)
    var = jnp.mean(jnp.square(xf - mu), axis=-1, keepdims=True)
    y = (xf - mu) * lax.rsqrt(var + EPS)
    return (y * g.astype(jnp.float32) + b.astype(jnp.float32)).astype(x.dtype)


def swiglu(h, w_in, w_out):
    gate, up = jnp.split(h @ w_in, 2, axis=-1)
    return (jax.nn.silu(gate) * up) @ w_out


def gmlp_spatial_gate(z, ln_g, ln_b, w_sp, b_sp):
    B, T, _ = z.shape
    z = jax.nn.gelu(z, approximate=False)
    u, v = z[..., :W_A], z[..., W_A:]
    v = layer_norm(v, ln_g, ln_b)
    v = v.reshape(B, T // CHUNK, CHUNK, A_GROUPS, W_A // A_GROUPS)
    mask = jnp.tril(jnp.ones((CHUNK, CHUNK), dtype=bool))
    w_m = jnp.where(mask[None], w_sp, jnp.zeros_like(w_sp))
    s = jnp.einsum('gij,bnjgc->bnigc', w_m, v) + b_sp.T[None, None, :, :, None]
    return u * s.reshape(B, T, W_A)


def causal_depthwise_conv(x, w, b):
    C = x.shape[-1]
    y = lax.conv_general_dilated(
        x, w[:, None, :].astype(x.dtype), window_strides=(1,),
        padding=[(CONV_WIDTH - 1, 0)], dimension_numbers=('NWC', 'WIO', 'NWC'),
        feature_group_count=C)
    return y + b


def conformer_conv(z, conv_w, conv_b, ln_g, ln_b):
    a, g = jnp.split(z, 2, axis=-1)
    h = a * jax.nn.sigmoid(g)
    h = causal_depthwise_conv(h, conv_w, conv_b)
    h = layer_norm(h, ln_g, ln_b)
    return jax.nn.silu(h)


def multiscale_pool(p, w_pool, scale):
    B, T, _ = p.shape
    pg = p.reshape(B, T, len(POOL_WINDOWS), POOL_GROUP).astype(jnp.float32)
    cs = jnp.cumsum(pg, axis=1)
    pos = jnp.arange(T)
    means = []
    for i, w in enumerate(POOL_WINDOWS):
        c = cs[:, :, i]
        prev = jnp.pad(c, ((0, 0), (w, 0), (0, 0)))[:, :T]
        cnt = jnp.minimum(pos + 1, w).astype(jnp.float32)[None, :, None]
        means.append((c - prev) / cnt)
    d = (jnp.stack(means, axis=2) - pg).astype(p.dtype)
    y = jnp.einsum('btgc,gce->btge', d, w_pool).reshape(B, T, W_C)
    return y * scale


def t5_bucket(n):
    n = jnp.maximum(n, 0)
    max_exact = REL_BUCKETS // 2
    nf = jnp.maximum(n, 1).astype(jnp.float32)
    large = max_exact + (jnp.log(nf / max_exact) / math.log(REL_MAX_DIST / max_exact)
                         * (REL_BUCKETS - max_exact)).astype(jnp.int32)
    large = jnp.minimum(large, REL_BUCKETS - 1)
    return jnp.where(n < max_exact, n, large)


def dsa_attention(q, q_idx, w_idx, c_kv, k_idx, w_uk, w_uv, rel_bias):
    B, T = q.shape[0], q.shape[1]
    nb = T // QUERY_BLOCK
    topk = min(TOPK_MAX, T // 4)
    spos = jnp.arange(T)
    scale = D_HEAD_DIM ** -0.5

    def to_blocks(a):
        return a.reshape((B, nb, QUERY_BLOCK) + a.shape[2:]).swapaxes(0, 1)

    def one_block(args):
        qb, qib, wb, t0 = args
        tpos = t0 + jnp.arange(QUERY_BLOCK)
        idx_logits = jnp.einsum('bqhd,bsd->bqhs', qib, k_idx)
        score = jnp.einsum('bqhs,bqh->bqs', jax.nn.relu(idx_logits).astype(jnp.float32),
                           wb.astype(jnp.float32))
        score = jnp.where(spos[None, None, :] <= tpos[None, :, None], score, -jnp.inf)
        _, sel = lax.top_k(score, topk)
        valid = sel <= tpos[None, :, None]
        kv = jax.vmap(lambda c, i: c[i])(c_kv, sel)
        q_lat = jnp.einsum('bqhd,hdr->bqhr', qb, w_uk)
        logits = jnp.einsum('bqhr,bqkr->bqkh', q_lat, kv).astype(jnp.float32) * scale
        bucket = t5_bucket(tpos[None, :, None] - sel)
        logits = logits + rel_bias[bucket].astype(jnp.float32)
        logits = jnp.where(valid[..., None], logits, -jnp.inf)
        p = jax.nn.softmax(logits, axis=2).astype(kv.dtype)
        o_lat = jnp.einsum('bqkh,bqkr->bqhr', p, kv)
        return jnp.einsum('bqhr,hrd->bqhd', o_lat, w_uv)

    out = lax.map(one_block, (to_blocks(q), to_blocks(q_idx), to_blocks(w_idx),
                              jnp.arange(nb, dtype=jnp.int32) * QUERY_BLOCK))
    return out.swapaxes(0, 1).reshape(B, T, W_D)


def mixer_ab(h, w_in, ln_a_g, ln_a_b, w_sp, b_sp, conv_w, conv_b, ln_b_g, ln_b_b, w_out):
    z = h @ w_in
    y_a = gmlp_spatial_gate(z[..., :2 * W_A], ln_a_g, ln_a_b, w_sp, b_sp)
    y_b = conformer_conv(z[..., 2 * W_A:], conv_w, conv_b, ln_b_g, ln_b_b)
    return jnp.concatenate([y_a, y_b], axis=-1) @ w_out


def mixer_cd(h, w_in, w_pool, pool_scale, g_cq, w_uq, w_qidx, g_ckv, w_uk, w_uv, rel_bias, w_out):
    B, T, _ = h.shape
    z = h @ w_in
    o1 = W_C
    o2 = o1 + Q_RANK
    o3 = o2 + KV_RANK
    o4 = o3 + IDX_DIM
    y_c = multiscale_pool(z[..., :o1], w_pool, pool_scale)
    c_q = rms_norm(z[..., o1:o2], g_cq)
    c_kv = rms_norm(z[..., o2:o3], g_ckv)
    k_idx = z[..., o3:o4]
    w_idx = z[..., o4:] * (IDX_HEADS ** -0.5 * IDX_DIM ** -0.5)
    q = (c_q @ w_uq).reshape(B, T, D_HEADS, D_HEAD_DIM)
    q_idx = (c_q @ w_qidx).reshape(B, T, IDX_HEADS, IDX_DIM)
    y_d = dsa_attention(q, q_idx, w_idx, c_kv, k_idx, w_uk, w_uv, rel_bias)
    return jnp.concatenate([y_c, y_d], axis=-1) @ w_out


def setup_inputs(seed: int = 0) -> dict:
    key = jax.random.key(seed)
    ks = jax.random.split(key, 32)
    f32 = jnp.float32

    def nrm(k, shape, fan):
        return jax.random.normal(k, shape, f32) * (fan ** -0.5)

    def gain(k, shape):
        return 1.0 + 0.05 * jax.random.normal(k, shape, f32)

    def bias(k, shape):
        return 0.02 * jax.random.normal(k, shape, f32)

    return {
        'x': jax.random.normal(ks[0], (BATCH, SEQ, D_MODEL), f32),
        'g_ff': gain(ks[1], (DEPTH, 2, D_MODEL)),
        'w_ff_in': nrm(ks[2], (DEPTH, 2, D_MODEL, 2 * D_FF), D_MODEL),
        'w_ff_out': nrm(ks[3], (DEPTH, 2, D_FF, D_MODEL), D_FF),
        'g_mix': gain(ks[4], (DEPTH, D_MODEL)),
        'w_in_ab': nrm(ks[5], (N_EVEN, D_MODEL, AB_IN), D_MODEL),
        'ln_a_g': gain(ks[6], (N_EVEN, W_A)),
        'ln_a_b': bias(ks[7], (N_EVEN, W_A)),
        'w_sp': nrm(ks[8], (N_EVEN, A_GROUPS, CHUNK, CHUNK), CHUNK),
        'b_sp': gain(ks[9], (N_EVEN, A_GROUPS, CHUNK)),
        'conv_w': nrm(ks[10], (N_EVEN, CONV_WIDTH, W_B), CONV_WIDTH),
        'conv_b': bias(ks[11], (N_EVEN, W_B)),
        'ln_b_g': gain(ks[12], (N_EVEN, W_B)),
        'ln_b_b': bias(ks[13], (N_EVEN, W_B)),
        'w_out_ab': nrm(ks[14], (N_EVEN, W_A + W_B, D_MODEL), W_A + W_B),
        'w_in_cd': nrm(ks[15], (N_ODD, D_MODEL, CD_IN), D_MODEL),
        'w_pool': nrm(ks[16], (N_ODD, len(POOL_WINDOWS), POOL_GROUP, POOL_GROUP), POOL_GROUP),
        'pool_scale': gain(ks[17], (N_ODD, W_C)),
        'g_cq': gain(ks[18], (N_ODD, Q_RANK)),
        'w_uq': nrm(ks[19], (N_ODD, Q_RANK, W_D), Q_RANK),
        'w_qidx': nrm(ks[20], (N_ODD, Q_RANK, IDX_HEADS * IDX_DIM), Q_RANK),
        'g_ckv': gain(ks[21], (N_ODD, KV_RANK)),
        'w_uk': nrm(ks[22], (N_ODD, D_HEADS, D_HEAD_DIM, KV_RANK), KV_RANK),
        'w_uv': nrm(ks[23], (N_ODD, D_HEADS, KV_RANK, D_HEAD_DIM), KV_RANK),
        'rel_bias': 0.5 * jax.random.normal(ks[24], (REL_BUCKETS, D_HEADS), f32),
        'w_out_cd': nrm(ks[25], (N_ODD, W_C + W_D, D_MODEL), W_C + W_D),
        'g_final': gain(ks[26], (D_MODEL,)),
    }


def reference(x, g_ff, w_ff_in, w_ff_out, g_mix, w_in_ab, ln_a_g, ln_a_b, w_sp, b_sp,
              conv_w, conv_b, ln_b_g, ln_b_b, w_out_ab, w_in_cd, w_pool, pool_scale,
              g_cq, w_uq, w_qidx, g_ckv, w_uk, w_uv, rel_bias, w_out_cd, g_final):
    for l in range(DEPTH):
        x = x + 0.5 * swiglu(rms_norm(x, g_ff[l, 0]), w_ff_in[l, 0], w_ff_out[l, 0])
        h = rms_norm(x, g_mix[l])
        i = l // 2
        if l % 2 == 0:
            x = x + mixer_ab(h, w_in_ab[i], ln_a_g[i], ln_a_b[i], w_sp[i], b_sp[i],
                             conv_w[i], conv_b[i], ln_b_g[i], ln_b_b[i], w_out_ab[i])
        else:
            x = x + mixer_cd(h, w_in_cd[i], w_pool[i], pool_scale[i], g_cq[i], w_uq[i],
                             w_qidx[i], g_ckv[i], w_uk[i], w_uv[i], rel_bias, w_out_cd[i])
        x = x + 0.5 * swiglu(rms_norm(x, g_ff[l, 1]), w_ff_in[l, 1], w_ff_out[l, 1])
    return rms_norm(x, g_final)
```

```python
import numpy as np
from contextlib import ExitStack
import concourse.bass as bass
import concourse.mybir as mybir
from concourse.bass_utils import run_bass_kernel_spmd

F32 = mybir.dt.float32
BF16 = mybir.dt.bfloat16
AF = mybir.ActivationFunctionType
ALU = mybir.AluOpType
AX = mybir.AxisListType

D = 4096
NKC = 32
TOK = 1024
DFF = 11008
NFC = 86
EPS = 1e-6
G4 = [[0, 1, 2, 3], [4, 5, 6, 7]]
G2 = [[0, 4], [1, 5], [2, 6], [3, 7]]


class Buf:
    __slots__ = ("name", "w", "r")

    def __init__(self, name=""):
        self.name = name
        self.w = None
        self.r = {}


class T:
    __slots__ = ("t", "b")

    def __init__(self, t, b=None):
        self.t = t
        self.b = b if b is not None else Buf()


class KB:
    ENG = ("pe", "act", "dve", "pool", "sp")

    def __init__(self, nc, es):
        self.nc = nc
        self.es = es
        self.E = {"pe": nc.tensor, "act": nc.scalar, "dve": nc.vector, "pool": nc.gpsimd, "sp": nc.sync}
        self.S = {}
        self.cnt = {}
        for e in self.ENG:
            self.S[e] = es.enter_context(nc.semaphore("s_" + e))
            self.cnt[e] = 0
        self.ND = 4
        self.dq = {}
        for q in ("sp", "act", "pool"):
            self.dq[q] = 0
            for i in range(self.ND):
                key = "d_%s%d" % (q, i)
                self.S[key] = es.enter_context(nc.semaphore(key))
                self.cnt[key] = 0
        self.ccn = {"A": 4, "B": 8, "X": 1}
        self.cci = {"A": 0, "B": 0, "X": 0}
        for pool_, n in self.ccn.items():
            for i in range(n):
                key = "cc%s%d" % (pool_, i)
                self.S[key] = es.enter_context(nc.semaphore(key))
                self.cnt[key] = 0
        self.seen = {e: {} for e in self.ENG}
        self.bgq = []

    def sb(self, es, name, shape, dt):
        self.uid = getattr(self, "uid", 0) + 1
        name = "%s_%d" % (name, self.uid)
        return T(es.enter_context(self.nc.sbuf_tensor(name, list(shape), dt)), Buf(name))

    def _waits(self, e, reads, writes, self_sync=True):
        need = {}

        def add(dep):
            if dep is None:
                return
            key, val = dep
            if key == e and not self_sync:
                return
            if key.startswith("cc"):
                val = self.cnt[key]
            if need.get(key, 0) < val:
                need[key] = val

        for b in reads:
            add(b.w)
        for b in writes:
            add(b.w)
            for d in b.r.items():
                add(d)
        eng = self.E[e]
        sn = self.seen[e]
        for key, val in need.items():
            if sn.get(key, 0) >= val:
                continue
            eng.wait_ge(self.S[key], val)
            sn[key] = val

    def _mark(self, me, reads, writes):
        key, val = me
        for b in reads:
            if b.r.get(key, 0) < val:
                b.r[key] = val
        for b in writes:
            b.w = me
            b.r = {}

    def op(self, e, build, reads=(), writes=(), self_sync=True):
        self._waits(e, reads, writes, self_sync)
        ins = build(self.E[e])
        self.cnt[e] += 1
        ins.then_inc(self.S[e], 1)
        self._mark((e, self.cnt[e]), reads, writes)
        return ins

    def dma(self, q, out, in_, reads=(), writes=(), **kw):
        self._waits(q, reads, writes)
        key = "d_%s%d" % (q, self.dq[q] % self.ND)
        self.dq[q] += 1
        self.cnt[key] += 16
        self.E[q].dma_start(out=out, in_=in_, **kw).then_inc(self.S[key], 16)
        self._mark((key, self.cnt[key]), reads, writes)

    def cc(self, kind, groups, in_ap, out_ap, reads=(), writes=(), pool="X"):
        self._waits("pool", reads, writes)
        key = "cc%s%d" % (pool, self.cci[pool] % self.ccn[pool])
        self.cci[pool] += 1
        self.cnt[key] += 1
        self.nc.gpsimd.collective_compute(kind, ALU.bypass, replica_groups=groups,
                                          ins=[in_ap], outs=[out_ap]).then_inc(self.S[key], 1)
        self._mark((key, self.cnt[key]), reads, writes)

    def barrier(self):
        for e in self.ENG:
            sn = self.seen[e]
            for key, val in self.cnt.items():
                if key == e or val == 0:
                    continue
                if sn.get(key, 0) >= val:
                    continue
                self.E[e].wait_ge(self.S[key], val)
                sn[key] = val

    def bg(self, n=1):
        for _ in range(n):
            if not self.bgq:
                return
            key, fn = self.bgq.pop(0)
            fn()

    def bg_flush_to(self, key):
        last = -1
        for i, (k, _) in enumerate(self.bgq):
            if k == key:
                last = i
        for _ in range(last + 1):
            k, fn = self.bgq.pop(0)
            fn()


class Ctx:
    pass


def rows_split(n, k):
    base = n // k
    return [(i * base, base) for i in range(k)]


def piece_rows(rows, cols):
    rs = rows // 8
    best = 1
    for m in range(1, rs + 1):
        if rs % m == 0 and m * cols * 2 * 8 <= (4 << 20):
            best = m
    return best


def shard_weight(w, c):
    rows, cols = w.shape
    m = piece_rows(rows, cols)
    P = rows // (8 * m)
    return np.ascontiguousarray(w.reshape(P, 8, m, cols)[:, c].reshape(P * m, cols))


class Wt:
    def __init__(self, t, pb, rpp, name):
        self.t = t
        self.pb = pb
        self.rpp = rpp
        self.name = name
        self.b = pb[-1]
        self.steps = None
        self.ready_key = {}
        self.ready_buf = {}

    def need_rows(self, kb, r0, r1):
        p0, p1 = r0 // self.rpp, (r1 - 1) // self.rpp
        kb.bg_flush_to(self.ready_key[p1])
        return self.pb[p0:p1 + 1] + [self.ready_buf[p1]]

    def need_all(self, kb):
        pl = len(self.pb) - 1
        kb.bg_flush_to(self.ready_key[pl])
        self.b = self.ready_buf[pl]
        return self.pb

    def pieces_for_rows(self, r0, r1):
        return [(self, p) for p in range(r0 // self.rpp, (r1 - 1) // self.rpp + 1)]

    def all_pieces(self):
        return [(self, p) for p in range(len(self.pb))]


def make_weight(kb, nc, name, rows, cols, tag=None):
    rs = rows // 8
    m = piece_rows(rows, cols)
    P = rs // m
    src = nc.dram_tensor(name, [rs, cols], F32, kind="ExternalInput").ap()
    wb = nc.dram_tensor(name + "_b", [rs, cols], BF16).ap()
    wh = nc.dram_tensor(name + "_h", [P * 4 * m, cols], BF16).ap()
    wf = nc.dram_tensor(name + "_f", [rows, cols], BF16).ap()
    pb = [Buf() for _ in range(P)]
    w = Wt(wf, pb, 8 * m, name)
    steps = []
    for p in range(P):
        bb, bh = Buf(), Buf()

        def f1(p=p, bb=bb):
            kb.dma("pool", wb[p * m:(p + 1) * m, :], src[p * m:(p + 1) * m, :], reads=[], writes=[bb])

        def f2(p=p, bb=bb, bh=bh):
            kb.cc("AllGather", G4, wb[p * m:(p + 1) * m, :], wh[p * 4 * m:(p + 1) * 4 * m, :], reads=[bb], writes=[bh], pool="A")

        def f3(p=p, bh=bh):
            kb.cc("AllGather", G2, wh[p * 4 * m:(p + 1) * 4 * m, :], wf[p * 8 * m:(p + 1) * 8 * m, :], reads=[bh], writes=[pb[p]], pool="B")
        steps.append((f1, f2, f3))
    w.steps = steps
    return w


def enqueue_pieces(kb, seq):
    seen = set()
    order = []
    for w, p in seq:
        if (w.name, p) in seen:
            continue
        seen.add((w.name, p))
        order.append((w, p))
    n = len(order)
    for i, (w, p) in enumerate(order):
        w.ready_key[p] = (order[min(i + 1, n - 1)][0].name, order[min(i + 1, n - 1)][1])
        w.ready_buf[p] = order[min(i + 1, n - 1)][0].pb[order[min(i + 1, n - 1)][1]]
    for i in range(n + 2):
        if i < n:
            w, p = order[i]
            kb.bgq.append((None, w.steps[p][0]))
        if 0 <= i - 1 < n:
            w, p = order[i - 1]
            kb.bgq.append((None, w.steps[p][1]))
        if 0 <= i - 2 < n:
            w, p = order[i - 2]
            kb.bgq.append(((w.name, p), w.steps[p][2]))


def load_x(kb, C, x_in):
    nc = kb.nc
    with ExitStack() as es:
        xt = [kb.sb(es, "lx_xt%d" % i, [128, D], F32) for i in range(2)]
        st = [kb.sb(es, "lx_st%d" % i, [128, NKC, 128], F32) for i in range(2)]
        n = 0
        for tt in range(TOK // 128):
            a = xt[tt % 2]
            kb.dma("sp", a.t[:], x_in[tt * 128:(tt + 1) * 128, :], writes=[a.b])
            s = st[tt % 2]
            for k4 in range(NKC // 4):
                ps = C.ps[n % 8]
                n += 1

                def mm(e):
                    ins = None
                    for j in range(4):
                        kc = k4 * 4 + j
                        ins = e.transpose(ps.t[:, j * 128:(j + 1) * 128], a.t[:, kc * 128:(kc + 1) * 128], C.ident.t[:])
                    return ins
                kb.op("pe", mm, reads=[a.b, C.ident.b], writes=[ps.b], self_sync=False)
                eng = "act" if k4 % 2 else "dve"
                o = s.t[:, k4 * 4:(k4 + 1) * 4, :]
                i_ = ps.t[:].rearrange("p (j t) -> p j t", j=4)
                if eng == "act":
                    kb.op("act", lambda e: e.copy(o, i_), reads=[ps.b], writes=[s.b])
                else:
                    kb.op("dve", lambda e: e.tensor_copy(o, i_), reads=[ps.b], writes=[s.b])
            kb.dma("sp", C.xT[:, :, tt * 128:(tt + 1) * 128].rearrange("k p t -> p k t"), s.t[:],
                   reads=[s.b], writes=[b for kc in range(NKC) for b in C.xTb[kc]])
        kb.barrier()


def final_norm_store(kb, C, gT, out_ap, do_norm=True):
    with ExitStack() as es:
        rstd = kb.sb(es, "fn_rstd", [128, TOK], F32)
        if do_norm:
            rms_stats(kb, C, es, rstd)
        xc = [kb.sb(es, "fn_xc%d" % i, [128, TOK], F32) for i in range(3)]
        xn = [kb.sb(es, "fn_xn%d" % i, [128, TOK], F32) for i in range(2)]
        st = [kb.sb(es, "fn_st%d" % i, [128, 8, 512], F32) for i in range(2)]
        n = 0
        for k4 in range(NKC // 4):
            s = st[k4 % 2]
            for j in range(4):
                kc = k4 * 4 + j
                a = xc[kc % 3]
                kb.dma("sp", a.t[:], C.xT[kc], reads=C.xTb[kc], writes=[a.b])
                if do_norm:
                    y = xn[kc % 2]
                    kb.op("dve", lambda e: e.scalar_tensor_tensor(out=y.t[:], in0=a.t[:], scalar=gT.t[:, kc:kc + 1],
                                                                  in1=rstd.t[:], op0=ALU.mult, op1=ALU.mult),
                          reads=[a.b, gT.b, rstd.b], writes=[y.b])
                else:
                    y = a
                for t2 in range(2):
                    ps = C.ps[n % 8]
                    n += 1

                    def mm(e):
                        ins = None
                        for q in range(4):
                            tt = t2 * 4 + q
                            ins = e.transpose(ps.t[:, q * 128:(q + 1) * 128], y.t[:, tt * 128:(tt + 1) * 128], C.ident.t[:])
                        return ins
                    kb.op("pe", mm, reads=[y.b, C.ident.b], writes=[ps.b], self_sync=False)
                    o = s.t[:, t2 * 4:(t2 + 1) * 4, j * 128:(j + 1) * 128]
                    i_ = ps.t[:].rearrange("p (q d) -> p q d", q=4)
                    if (j + t2) % 2:
                        kb.op("act", lambda e: e.copy(o, i_), reads=[ps.b], writes=[s.b])
                    else:
                        kb.op("dve", lambda e: e.tensor_copy(o, i_), reads=[ps.b], writes=[s.b])
            kb.dma("sp", out_ap[:, k4 * 512:(k4 + 1) * 512].rearrange("(t p) d -> p t d", p=128), s.t[:],
                   reads=[s.b], writes=[C.outb])
        kb.barrier()


def rms_stats(kb, C, es, rstd):
    xc = [kb.sb(es, "rs_xc%d" % i, [128, TOK], F32) for i in range(3)]
    sq = [kb.sb(es, "rs_sq%d" % i, [128, TOK], F32) for i in range(2)]
    p0, p1 = C.ps[0], C.ps[1]
    for kc in range(NKC):
        a = xc[kc % 3]
        kb.dma("sp", a.t[:], C.xT[kc], reads=C.xTb[kc], writes=[a.b])
        s = sq[kc % 2]
        kb.op("act", lambda e: e.activation(out=s.t[:], in_=a.t[:], func=AF.Square), reads=[a.b], writes=[s.b])
        for th, ps in enumerate((p0, p1)):
            kb.op("pe", lambda e: e.matmul(ps.t[:], C.ones.t[:], s.t[:, th * 512:(th + 1) * 512],
                                           start=(kc == 0), stop=(kc == NKC - 1)),
                  reads=[s.b, C.ones.b], writes=[ps.b], self_sync=False)
    for th, ps in enumerate((p0, p1)):
        kb.op("act", lambda e: e.activation(out=rstd.t[:, th * 512:(th + 1) * 512], in_=ps.t[:], func=AF.Sqrt,
                                            bias=EPS, scale=1.0 / D),
              reads=[ps.b], writes=[rstd.b])
    kb.op("dve", lambda e: e.reciprocal(out=rstd.t[:], in_=rstd.t[:]), reads=[rstd.b], writes=[rstd.b])


def rmsnorm_to_hT(kb, C, gT):
    with ExitStack() as es:
        rstd = kb.sb(es, "rn_rstd", [128, TOK], F32)
        rms_stats(kb, C, es, rstd)
        xc = [kb.sb(es, "rn_xc%d" % i, [128, TOK], F32) for i in range(3)]
        for kc in range(NKC):
            a = xc[kc % 3]
            kb.dma("sp", a.t[:], C.xT[kc], reads=C.xTb[kc], writes=[a.b])
            eng = "dve"
            kb.op(eng, lambda e: e.scalar_tensor_tensor(out=C.hT.t[:, kc, :], in0=a.t[:], scalar=gT.t[:, kc:kc + 1],
                                                        in1=rstd.t[:], op0=ALU.mult, op1=ALU.mult),
                  reads=[a.b, gT.b, rstd.b], writes=[C.hTb[kc]])
        kb.barrier()


FFN_GROUPS = [15, 15, 14, 14, 14, 14]


def ffn_stage(kb, C, gT, w_in, w_out, bg_per_group=0):
    with ExitStack() as es:
        C.hT = kb.sb(es, "f_hT", [128, NKC, TOK], BF16)
        C.hTb = [Buf() for _ in range(NKC)]
        rmsnorm_to_hT(kb, C, gT)
        hT = C.hT
        GM = max(FFN_GROUPS)
        aT = kb.sb(es, "f_aT", [128, GM, TOK], BF16)
        aTb = [Buf() for _ in range(GM)]
        wi = [(kb.sb(es, "f_wg%d" % i, [128, NKC, 128], BF16), kb.sb(es, "f_wu%d" % i, [128, NKC, 128], BF16)) for i in range(2)]
        wo = [kb.sb(es, "f_wo%d" % i, [128, GM, 512], BF16) for i in range(2)]
        sg = [kb.sb(es, "f_sg%d" % i, [128, 512], BF16) for i in range(2)]
        yb = [kb.sb(es, "f_y%d" % i, [128, 512], F32) for i in range(4)]
        it = 0
        ust = 0
        dstp = 0
        f0 = 0
        hreads = list(C.hTb)
        for G in FFN_GROUPS:
            for j in range(G):
                fc = f0 + j
                wg, wu = wi[it % 2]
                it += 1
                for s_, wt_ in ((0, wg), (1, wu)):
                    r0 = (2 * fc + s_) * 128
                    kb.dma("sp", wt_.t[:], w_in.t[r0:r0 + 128, :].rearrange("p (kc f) -> p kc f", kc=NKC),
                           reads=w_in.need_rows(kb, r0, r0 + 128), writes=[wt_.b])
                kb.bg(bg_per_group)
                for th in range(2):
                    pg = C.ps[(ust % 2) * 2]
                    pu = C.ps[(ust % 2) * 2 + 1]
                    s = sg[ust % 2]
                    ust += 1
                    tsl = slice(th * 512, (th + 1) * 512)

                    def mmg(e, w=wg, p=pg):
                        ins = None
                        for kc in range(NKC):
                            ins = e.matmul(p.t[:], w.t[:, kc, :], hT.t[:, kc, tsl], start=(kc == 0), stop=(kc == NKC - 1))
                        return ins
                    kb.op("pe", mmg, reads=[wg.b] + hreads, writes=[pg.b], self_sync=False)
                    kb.op("act", lambda e: e.activation(out=s.t[:], in_=pg.t[:], func=AF.Silu), reads=[pg.b], writes=[s.b])
                    kb.op("pe", lambda e: mmg(e, wu, pu), reads=[wu.b] + hreads, writes=[pu.b], self_sync=False)
                    kb.op("dve", lambda e: e.tensor_tensor(out=aT.t[:, j, tsl], in0=s.t[:], in1=pu.t[:], op=ALU.mult),
                          reads=[s.b, pu.b], writes=[aTb[j]])
            for db in range(8):
                w = wo[dstp % 2]
                kb.dma("sp", w.t[:, :G, :],
                       w_out.t[f0 * 128:(f0 + G) * 128, db * 512:(db + 1) * 512].rearrange("(g p) d -> p g d", p=128),
                       reads=w_out.need_rows(kb, f0 * 128, (f0 + G) * 128), writes=[w.b])
                for dcl in range(4):
                    dc = db * 4 + dcl
                    for th in range(2):
                        ps = C.ps[4 + dstp % 4]
                        y = yb[dstp % 4]
                        dstp += 1
                        tsl = slice(th * 512, (th + 1) * 512)

                        def mmd(e):
                            ins = None
                            for j in range(G):
                                ins = e.matmul(ps.t[:], w.t[:, j, dcl * 128:(dcl + 1) * 128], aT.t[:, j, tsl],
                                               start=(j == 0), stop=(j == G - 1))
                            return ins
                        kb.op("pe", mmd, reads=[w.b] + aTb[:G], writes=[ps.b], self_sync=False)
                        if dstp % 2:
                            kb.op("act", lambda e: e.activation(out=y.t[:], in_=ps.t[:], func=AF.Copy, scale=0.5),
                                  reads=[ps.b], writes=[y.b])
                        else:
                            kb.op("dve", lambda e: e.tensor_scalar(out=y.t[:], in0=ps.t[:], scalar1=0.5, scalar2=None, op0=ALU.mult),
                                  reads=[ps.b], writes=[y.b])
                        kb.dma("pool", C.xT[dc][:, tsl], y.t[:], accum_op=ALU.add, reads=[y.b], writes=[C.xTb[dc][th]])
                kb.bg(bg_per_group)
            f0 += G
        kb.barrier()


def rms_hT_n(kb, C, src_fn, N, gT, hT, hTb, col0, tagp):
    with ExitStack() as es:
        rstd = kb.sb(es, tagp + "_rstd", [128, N], F32)
        xc = [kb.sb(es, tagp + "_xc%d" % i, [128, N], F32) for i in range(3)]
        sq = [kb.sb(es, tagp + "_sq%d" % i, [128, N], F32) for i in range(2)]
        ps = C.ps[0]

        def get(kc):
            ap, bufs, is_dram = src_fn(kc)
            if is_dram:
                a = xc[kc % 3]
                kb.dma("sp", a.t[:], ap, reads=bufs, writes=[a.b])
                return a.t[:], [a.b]
            return ap, bufs
        for kc in range(NKC):
            xin, xb = get(kc)
            s = sq[kc % 2]
            kb.op("act", lambda e: e.activation(out=s.t[:], in_=xin, func=AF.Square), reads=xb, writes=[s.b])
            kb.op("pe", lambda e: e.matmul(ps.t[:, :N], C.ones.t[:], s.t[:], start=(kc == 0), stop=(kc == NKC - 1)),
                  reads=[s.b, C.ones.b], writes=[ps.b], self_sync=False)
        kb.op("act", lambda e: e.activation(out=rstd.t[:], in_=ps.t[:, :N], func=AF.Sqrt, bias=EPS, scale=1.0 / D),
              reads=[ps.b], writes=[rstd.b])
        kb.op("dve", lambda e: e.reciprocal(out=rstd.t[:], in_=rstd.t[:]), reads=[rstd.b], writes=[rstd.b])
        for kc in range(NKC):
            xin, xb = get(kc)
            kb.op("dve", lambda e: e.scalar_tensor_tensor(out=hT.t[:, kc, col0:col0 + N], in0=xin, scalar=gT.t[:, kc:kc + 1],
                                                          in1=rstd.t[:], op0=ALU.mult, op1=ALU.mult),
                  reads=xb + [gT.b, rstd.b], writes=[hTb[kc]])
        kb.barrier()


def load_wtile(kb, w, col, width, wt):
    kb.bg(1)
    kb.dma("sp", wt.t[:, :, :width], w.t[:, col:col + width].rearrange("(kc p) f -> p kc f", p=128),
           reads=[w.b], writes=[wt.b])


def mm_proj(kb, ps, width, N, wt, hT, hTb, c0):
    def mm(e):
        ins = None
        for kc in range(NKC):
            ins = e.matmul(ps.t[:width, :N], wt.t[:, kc, :width], hT.t[:, kc, c0:c0 + N], start=(kc == 0), stop=(kc == NKC - 1))
        return ins
    kb.op("pe", mm, reads=[wt.b] + list(hTb), writes=[ps.b], self_sync=False)


def ln_fm(kb, C, y, ybufs, NCH, N, gT, bT, out_fn, func, tagp):
    NC = NCH * 128
    with ExitStack() as es:
        sq = [kb.sb(es, tagp + "_sq%d" % i, [128, N], F32) for i in range(2)]
        tm = [kb.sb(es, tagp + "_tm%d" % i, [128, N], F32) for i in range(2)]
        mean = kb.sb(es, tagp + "_mean", [128, N], F32)
        rstd = kb.sb(es, tagp + "_rstd", [128, N], F32)
        msq = kb.sb(es, tagp + "_msq", [128, N], F32)
        p1, p2 = C.ps[0], C.ps[1]
        for fc in range(NCH):
            kb.op("pe", lambda e: e.matmul(p1.t[:, :N], C.ones.t[:], y.t[:, fc, :], start=(fc == 0), stop=(fc == NCH - 1)),
                  reads=[ybufs[fc], C.ones.b], writes=[p1.b], self_sync=False)
            s = sq[fc % 2]
            kb.op("act", lambda e: e.activation(out=s.t[:], in_=y.t[:, fc, :], func=AF.Square), reads=[ybufs[fc]], writes=[s.b])
            kb.op("pe", lambda e: e.matmul(p2.t[:, :N], C.ones.t[:], s.t[:], start=(fc == 0), stop=(fc == NCH - 1)),
                  reads=[s.b, C.ones.b], writes=[p2.b], self_sync=False)
        kb.op("act", lambda e: e.activation(out=mean.t[:], in_=p1.t[:, :N], func=AF.Copy, scale=1.0 / NC), reads=[p1.b], writes=[mean.b])
        kb.op("dve", lambda e: e.tensor_tensor(out=msq.t[:], in0=mean.t[:], in1=mean.t[:], op=ALU.mult), reads=[mean.b], writes=[msq.b])
        kb.op("dve", lambda e: e.scalar_tensor_tensor(out=rstd.t[:], in0=p2.t[:, :N], scalar=1.0 / NC, in1=msq.t[:],
                                                      op0=ALU.mult, op1=ALU.subtract), reads=[p2.b, msq.b], writes=[rstd.b])
        kb.op("act", lambda e: e.activation(out=rstd.t[:], in_=rstd.t[:], func=AF.Sqrt, bias=EPS, scale=1.0), reads=[rstd.b], writes=[rstd.b])
        kb.op("dve", lambda e: e.reciprocal(out=rstd.t[:], in_=rstd.t[:]), reads=[rstd.b], writes=[rstd.b])
        for fc in range(NCH):
            t = tm[fc % 2]
            kb.op("dve", lambda e: e.tensor_tensor(out=t.t[:], in0=y.t[:, fc, :], in1=mean.t[:], op=ALU.subtract),
                  reads=[ybufs[fc], mean.b], writes=[t.b])
            kb.op("pool", lambda e: e.tensor_tensor(out=t.t[:], in0=t.t[:], in1=rstd.t[:], op=ALU.mult),
                  reads=[t.b, rstd.b], writes=[t.b])
            o_ap, o_buf = out_fn(fc)
            kb.op("act", lambda e: e.activation(out=o_ap, in_=t.t[:], func=func, scale=gT.t[:, fc:fc + 1], bias=bT.t[:, fc:fc + 1]),
                  reads=[t.b, gT.b, bT.b], writes=[o_buf])
        kb.barrier()


def out_proj(kb, C, w, rhs_fn, n0, N, tagp, nk=NKC, krow0=0):
    with ExitStack() as es:
        wo = [kb.sb(es, tagp + "_wo%d" % i, [128, nk, 256], BF16) for i in range(2)]
        yb = [kb.sb(es, tagp + "_y%d" % i, [128, N], F32) for i in range(4)]
        rh = [rhs_fn(fc) for fc in range(nk)]
        cnt = 0
        for blk in range(16):
            kb.bg(1)
            wt = wo[blk % 2]
            kb.dma("sp", wt.t[:], w.t[krow0 * 128:(krow0 + nk) * 128, blk * 256:(blk + 1) * 256].rearrange("(g p) d -> p g d", p=128),
                   reads=[w.b], writes=[wt.b])
            for dcl in range(2):
                dc = blk * 2 + dcl
                ps = C.ps[4 + cnt % 4]
                y = yb[cnt % 4]
                cnt += 1

                def mm(e):
                    ins = None
                    for fc in range(nk):
                        ins = e.matmul(ps.t[:, :N], wt.t[:, fc, dcl * 128:(dcl + 1) * 128], rh[fc][0], start=(fc == 0), stop=(fc == nk - 1))
                    return ins
                kb.op("pe", mm, reads=[wt.b] + [r[1] for r in rh], writes=[ps.b], self_sync=False)
                if cnt % 2:
                    kb.op("act", lambda e: e.copy(y.t[:], ps.t[:, :N]), reads=[ps.b], writes=[y.b])
                else:
                    kb.op("dve", lambda e: e.tensor_copy(y.t[:], ps.t[:, :N]), reads=[ps.b], writes=[y.b])
                kb.dma("pool", C.xT[dc][:, n0:n0 + N], y.t[:], accum_op=ALU.add, reads=[y.b], writes=[C.xTb[dc][n0 // 512]])
        kb.barrier()


def gather_xtail(kb, C, nc, tag, ntail=32):
    src = nc.dram_tensor(tag + "_xtail_s", [D, ntail], F32).ap()
    allt = nc.dram_tensor(tag + "_xtail_a", [4 * D, ntail], F32).ap()
    sb_, ab_ = Buf(), Buf()
    for k8 in range(4):
        kb.dma("sp", src[k8 * 1024:(k8 + 1) * 1024, :].rearrange("(kc p) t -> kc p t", p=128),
               C.xT[k8 * 8:(k8 + 1) * 8, :, TOK - ntail:TOK],
               reads=[b for kc in range(k8 * 8, (k8 + 1) * 8) for b in C.xTb[kc]], writes=[sb_])
    kb.cc("AllGather", G4, src, allt, reads=[sb_], writes=[ab_])
    return allt, ab_


def halo_x(kb, C, es, allt, ab_, ntail, tagp):
    xt = kb.sb(es, tagp + "_xt4", [128, 4, NKC, ntail], F32)
    xh = kb.sb(es, tagp + "_xh", [128, NKC, ntail], F32)
    for q in range(4):
        kb.dma("sp", xt.t[:, q], allt[q * D:(q + 1) * D, :].rearrange("(kc p) t -> p kc t", p=128), reads=[ab_], writes=[xt.b])
    kb.op("dve", lambda e: e.tensor_scalar(out=xh.t[:], in0=xt.t[:, 0], scalar1=C.sel.t[:, 0:1], scalar2=None, op0=ALU.mult),
          reads=[xt.b, C.sel.b], writes=[xh.b])
    for q in range(1, 4):
        kb.op("dve", lambda e: e.scalar_tensor_tensor(out=xh.t[:], in0=xt.t[:, q], scalar=C.sel.t[:, q:q + 1], in1=xh.t[:],
                                                      op0=ALU.mult, op1=ALU.add),
              reads=[xt.b, C.sel.b, xh.b], writes=[xh.b])
    return xh


HB = 32
NH = 512
KW = 31


def mixer_ab(kb, C, nc, P, w_in, w_out):
    WA = 2048
    allt, ab_ = gather_xtail(kb, C, nc, "ab")
    with ExitStack() as es:
        hB = kb.sb(es, "ab_hB", [128, 16, HB + NH], BF16)
        hBb = [Buf() for _ in range(16)]
        yA = kb.sb(es, "ab_yA", [128, 16, NH], BF16)
        yAb = [Buf() for _ in range(16)]
        yB = kb.sb(es, "ab_yB", [128, 16, NH], BF16)
        yBb = [Buf() for _ in range(16)]
        zu = kb.sb(es, "ab_zu", [128, 16, NH], BF16)
        zub = [Buf() for _ in range(16)]
        zv = kb.sb(es, "ab_zv", [128, 16, NH], F32)
        zvb = [Buf() for _ in range(16)]
        WmT = kb.sb(es, "ab_WmT", [128, 8, 128], BF16)
        for g in range(8):
            kb.op("dve", lambda e: e.tensor_tensor(out=WmT.t[:, g, :], in0=P["wspT"].t[:, g, :], in1=P["tri"].t[:], op=ALU.mult),
                  reads=[P["wspT"].b, P["tri"].b], writes=[WmT.b])
        bank = [0]

        def nextps():
            p = C.ps[2 + bank[0] % 6]
            bank[0] += 1
            return p

        def glu_proj(hT, hTb, N, dst0, wr, sgr):
            for fc in range(16):
                wa = wr[(2 * fc) % 3]
                wg = wr[(2 * fc + 1) % 3]
                load_wtile(kb, w_in, 2 * WA + fc * 128, 128, wa)
                pa = nextps()
                mm_proj(kb, pa, 128, N, wa, hT, hTb, 0)
                load_wtile(kb, w_in, 3 * WA + fc * 128, 128, wg)
                pg = nextps()
                mm_proj(kb, pg, 128, N, wg, hT, hTb, 0)
                sg = sgr[fc % 2]
                kb.op("act", lambda e: e.activation(out=sg.t[:, :N], in_=pg.t[:, :N], func=AF.Sigmoid), reads=[pg.b], writes=[sg.b])
                kb.op("dve", lambda e: e.tensor_tensor(out=hB.t[:, fc, dst0:dst0 + N], in0=sg.t[:, :N], in1=pa.t[:, :N], op=ALU.mult),
                      reads=[sg.b, pa.b], writes=[hBb[fc]])

        with ExitStack() as es1:
            xh = halo_x(kb, C, es1, allt, ab_, HB, "abh")
            hTh = kb.sb(es1, "abh_hT", [128, NKC, HB], BF16)
            hThb = [Buf() for _ in range(NKC)]
            rms_hT_n(kb, C, lambda kc: (xh.t[:, kc, :], [xh.b], False), HB, P["gmix0"], hTh, hThb, 0, "abh_rn")
            wr = [kb.sb(es1, "abh_w%d" % i, [128, NKC, 128], BF16) for i in range(3)]
            sgr = [kb.sb(es1, "abh_sg%d" % i, [128, NH], F32) for i in range(2)]
            glu_proj(hTh, hThb, HB, 0, wr, sgr)
            kb.barrier()

        for half in range(2):
            n0 = half * NH
            with ExitStack() as es1:
                hT = kb.sb(es1, "ab_hT", [128, NKC, NH], BF16)
                hTb = [Buf() for _ in range(NKC)]
                rms_hT_n(kb, C, lambda kc: (C.xT[kc][:, n0:n0 + NH], C.xTb[kc], True), NH, P["gmix0"], hT, hTb, 0, "ab_rn")
                wr = [kb.sb(es1, "ab_w%d" % i, [128, NKC, 128], BF16) for i in range(3)]
                sgr = [kb.sb(es1, "ab_sg%d" % i, [128, NH], F32) for i in range(2)]
                glu_proj(hT, hTb, NH, HB, wr, sgr)
                for fc in range(16):
                    wu = wr[(2 * fc) % 3]
                    wv = wr[(2 * fc + 1) % 3]
                    load_wtile(kb, w_in, fc * 128, 128, wu)
                    pu = nextps()
                    mm_proj(kb, pu, 128, NH, wu, hT, hTb, 0)
                    kb.op("act", lambda e: e.activation(out=zu.t[:, fc, :], in_=pu.t[:, :NH], func=AF.Gelu), reads=[pu.b], writes=[zub[fc]])
                    load_wtile(kb, w_in, WA + fc * 128, 128, wv)
                    pv = nextps()
                    mm_proj(kb, pv, 128, NH, wv, hT, hTb, 0)
                    kb.op("act", lambda e: e.activation(out=zv.t[:, fc, :], in_=pv.t[:, :NH], func=AF.Gelu), reads=[pv.b], writes=[zvb[fc]])
                kb.barrier()
            ln_fm(kb, C, zv, zvb, 16, NH, P["lnag"], P["lnab"], lambda fc: (zv.t[:, fc, :], zvb[fc]), AF.Identity, "ab_lna")
            with ExitStack() as es1:
                vt = [kb.sb(es1, "ab_vt%d" % i, [128, 128], BF16) for i in range(3)]
                tt_ = [kb.sb(es1, "ab_t%d" % i, [128, 128], F32) for i in range(3)]
                n = 0
                for tq in range(NH // 128):
                    tsl = slice(tq * 128, (tq + 1) * 128)
                    for fc in range(16):
                        g = fc // 2
                        pt = nextps()
                        kb.op("pe", lambda e: e.transpose(pt.t[:, :128], zv.t[:, fc, tsl], C.ident.t[:]),
                              reads=[zvb[fc], C.ident.b], writes=[pt.b], self_sync=False)
                        v_ = vt[n % 3]
                        t_ = tt_[n % 3]
                        n += 1
                        kb.op("act", lambda e: e.copy(v_.t[:], pt.t[:, :128]), reads=[pt.b], writes=[v_.b])
                        ps = nextps()
                        kb.op("pe", lambda e: e.matmul(ps.t[:, :128], v_.t[:], WmT.t[:, g, :], start=True, stop=True),
                              reads=[v_.b, WmT.b], writes=[ps.b], self_sync=False)
                        kb.op("dve", lambda e: e.tensor_tensor(out=t_.t[:], in0=ps.t[:, :128], in1=P["bspbc"].t[:, g, :], op=ALU.add),
                              reads=[ps.b, P["bspbc"].b], writes=[t_.b])
                        kb.op("pool", lambda e: e.tensor_tensor(out=yA.t[:, fc, tsl], in0=t_.t[:], in1=zu.t[:, fc, tsl], op=ALU.mult),
                              reads=[t_.b, zub[fc]], writes=[yAb[fc]])
                kb.barrier()
            with ExitStack() as es1:
                for fc in range(16):
                    cw = P["convw"].t
                    kb.op("dve", lambda e: e.tensor_scalar(out=zv.t[:, fc, :], in0=hB.t[:, fc, 2:2 + NH], scalar1=cw[:, fc, 0:1],
                                                           scalar2=P["convb"].t[:, fc:fc + 1], op0=ALU.mult, op1=ALU.add),
                          reads=[hBb[fc], P["convw"].b, P["convb"].b], writes=[zvb[fc]])
                    for k in range(1, KW):
                        kb.op("dve", lambda e: e.scalar_tensor_tensor(out=zv.t[:, fc, :], in0=hB.t[:, fc, 2 + k:2 + k + NH], scalar=cw[:, fc, k:k + 1],
                                                                      in1=zv.t[:, fc, :], op0=ALU.mult, op1=ALU.add),
                              reads=[hBb[fc], zvb[fc]], writes=[zvb[fc]])
                    kb.op("pool", lambda e: e.tensor_copy(hB.t[:, fc, 0:HB], hB.t[:, fc, NH:NH + HB]), reads=[hBb[fc]], writes=[hBb[fc]])
            ln_fm(kb, C, zv, zvb, 16, NH, P["lnbg"], P["lnbb"], lambda fc: (yB.t[:, fc, :], yBb[fc]), AF.Silu, "ab_lnb")
            out_proj(kb, C, w_out, lambda fc: ((yA.t[:, fc, :], yAb[fc]) if fc < 16 else (yB.t[:, fc - 16, :], yBb[fc - 16])), n0, NH, "ab_o")
        kb.barrier()


NEG = -1.0e30
RNK = 1152
RNPAD = 512


def rms_fm(kb, C, z, zb, NCH, N, gT, out_fn, tagp):
    NC = NCH * 128
    with ExitStack() as es:
        sq = [kb.sb(es, tagp + "_sq%d" % i, [128, N], F32) for i in range(2)]
        rstd = kb.sb(es, tagp + "_rstd", [128, N], F32)
        p1 = C.ps[0]
        for fc in range(NCH):
            s = sq[fc % 2]
            kb.op("act", lambda e: e.activation(out=s.t[:], in_=z.t[:, fc, :], func=AF.Square), reads=[zb[fc]], writes=[s.b])
            kb.op("pe", lambda e: e.matmul(p1.t[:, :N], C.ones.t[:], s.t[:], start=(fc == 0), stop=(fc == NCH - 1)),
                  reads=[s.b, C.ones.b], writes=[p1.b], self_sync=False)
        kb.op("act", lambda e: e.activation(out=rstd.t[:], in_=p1.t[:, :N], func=AF.Sqrt, bias=EPS, scale=1.0 / NC), reads=[p1.b], writes=[rstd.b])
        kb.op("dve", lambda e: e.reciprocal(out=rstd.t[:], in_=rstd.t[:]), reads=[rstd.b], writes=[rstd.b])
        for fc in range(NCH):
            o_ap, o_buf = out_fn(fc)
            kb.op("dve", lambda e: e.scalar_tensor_tensor(out=o_ap, in0=z.t[:, fc, :], scalar=gT.t[:, fc:fc + 1], in1=rstd.t[:],
                                                          op0=ALU.mult, op1=ALU.mult), reads=[zb[fc], gT.b, rstd.b], writes=[o_buf])
        kb.barrier()


def mixer_cd(kb, C, nc, P, W):
    w_in, w_out, w_pool, w_uq, w_qidx, w_uk, w_uv = W
    SC = 128.0 ** -0.5
    WI = 32.0 ** -0.5 * 64.0 ** -0.5
    allt, ab_ = gather_xtail(kb, C, nc, "cd")
    bf = BF16
    ckvT_s = T(nc.dram_tensor("cd_ckvT_s", [512, TOK], bf).ap())
    ckvT_a = T(nc.dram_tensor("cd_ckvT_a", [4 * 512, TOK], bf).ap())
    ckvm_s = T(nc.dram_tensor("cd_ckvm_s", [TOK, 512], bf).ap())
    ckvm_a = T(nc.dram_tensor("cd_ckvm_a", [4 * TOK, 512], bf).ap())
    kidx_s = T(nc.dram_tensor("cd_kidx_s", [64, TOK], bf).ap())
    kidx_a = T(nc.dram_tensor("cd_kidx_a", [4 * 64, TOK], bf).ap())
    q_d = T(nc.dram_tensor("cd_q_d", [16, 128, TOK], bf).ap())
    qi_d = T(nc.dram_tensor("cd_qi_d", [16, 128, TOK], bf).ap())
    RN_t = nc.dram_tensor("cd_RN_d", [16, RNK], bf)
    RN_d = T(RN_t.ap())
    bank = [0]

    def nextps():
        p = C.ps[2 + bank[0] % 6]
        bank[0] += 1
        return p

    with ExitStack() as es:
        ckvT_own = kb.sb(es, "cd_ckvTo", [128, 4, TOK], bf)
        ckvTob = [Buf() for _ in range(4)]
        kidx_own = kb.sb(es, "cd_kidxo", [64, TOK], bf)
        w_tm = kb.sb(es, "cd_wtm", [128, 8, 32], F32)
        identb = kb.sb(es, "cd_identb", [128, 128], bf)
        onesb = kb.sb(es, "cd_onesb", [128, 128], bf)
        Jb = kb.sb(es, "cd_Jb", [128, 128], bf)
        kb.op("act", lambda e: e.copy(identb.t[:], C.ident.t[:]), reads=[C.ident.b], writes=[identb.b])
        kb.op("act", lambda e: e.copy(Jb.t[:], P["Jf"].t[:]), reads=[P["Jf"].b], writes=[Jb.b])
        kb.op("dve", lambda e: e.memset(onesb.t[:], 1.0), writes=[onesb.b])
        with ExitStack() as es1:
            rn = kb.sb(es1, "cd_rn", [16, RNK], bf)
            for ci, (c0, n) in enumerate(((0, 512), (512, 512), (1024, 128))):
                ps = nextps()
                kb.op("pe", lambda e: e.matmul(ps.t[:16, :n], P["rb"].t[0:32, :], P["OHk"].t[0:32, c0:c0 + n], start=True, stop=True),
                      reads=[P["rb"].b, P["OHk"].b], writes=[ps.b], self_sync=False)
                kb.op("act", lambda e: e.activation(out=rn.t[:, c0:c0 + n], in_=ps.t[:16, :n], func=AF.Copy, scale=1.0 / SC),
                      reads=[ps.b], writes=[rn.b])
            kb.dma("sp", RN_d.t[:, :], rn.t[:], reads=[rn.b], writes=[RN_d.b])
            kb.barrier()

        with ExitStack() as esP:
            pT = kb.sb(esP, "cd_pT", [128, 16, HB + NH], F32)
            pTb = [Buf() for _ in range(16)]
            wpool = kb.sb(esP, "cd_wpool", [128, 16, 512], bf)
            kb.dma("sp", wpool.t[:], w_pool.t[:, :].rearrange("(g p) e -> p g e", p=128), reads=[w_pool.b], writes=[wpool.b])

            def p_proj(hT, hTb, N, dst0, wr):
                for fc in range(16):
                    wt = wr[fc % 3]
                    load_wtile(kb, w_in, fc * 128, 128, wt)
                    ps = nextps()
                    mm_proj(kb, ps, 128, N, wt, hT, hTb, 0)
                    if fc % 2:
                        kb.op("act", lambda e: e.copy(pT.t[:, fc, dst0:dst0 + N], ps.t[:, :N]), reads=[ps.b], writes=[pTb[fc]])
                    else:
                        kb.op("dve", lambda e: e.tensor_copy(pT.t[:, fc, dst0:dst0 + N], ps.t[:, :N]), reads=[ps.b], writes=[pTb[fc]])

            with ExitStack() as es1:
                xh = halo_x(kb, C, es1, allt, ab_, HB, "cdh")
                hTh = kb.sb(es1, "cdh_hT", [128, NKC, HB], bf)
                hThb = [Buf() for _ in range(NKC)]
                rms_hT_n(kb, C, lambda kc: (xh.t[:, kc, :], [xh.b], False), HB, P["gmix1"], hTh, hThb, 0, "cdh_rn")
                wr = [kb.sb(es1, "cdh_w%d" % i, [128, NKC, 128], bf) for i in range(3)]
                p_proj(hTh, hThb, HB, 0, wr)
                kb.barrier()

            for half in range(2):
                n0 = half * NH
                with ExitStack() as es1:
                    hT = kb.sb(es1, "cd_hT", [128, NKC, NH], bf)
                    hTb = [Buf() for _ in range(NKC)]
                    rms_hT_n(kb, C, lambda kc: (C.xT[kc][:, n0:n0 + NH], C.xTb[kc], True), NH, P["gmix1"], hT, hTb, 0, "cd_rn")
                    wr = [kb.sb(es1, "cd_w%d" % i, [128, NKC, 128], bf) for i in range(3)]
                    zq = kb.sb(es1, "cd_zq", [128, 8, NH], F32)
                    zqb = [Buf() for _ in range(8)]
                    zkv = kb.sb(es1, "cd_zkv", [128, 4, NH], F32)
                    zkvb = [Buf() for _ in range(4)]
                    cqT = kb.sb(es1, "cd_cqT", [128, 8, NH], bf)
                    cqb = [Buf() for _ in range(8)]
                    p_proj(hT, hTb, NH, HB, wr)
                    for fc in range(12):
                        wt = wr[fc % 3]
                        load_wtile(kb, w_in, 2048 + fc * 128, 128, wt)
                        ps = nextps()
                        mm_proj(kb, ps, 128, NH, wt, hT, hTb, 0)
                        dst, db_ = (zq.t[:, fc, :], zqb[fc]) if fc < 8 else (zkv.t[:, fc - 8, :], zkvb[fc - 8])
                        kb.op("act", lambda e: e.copy(dst, ps.t[:, :NH]), reads=[ps.b], writes=[db_])
                    wt = wr[0]
                    load_wtile(kb, w_in, 3584, 96, wt)
                    ps = nextps()
                    mm_proj(kb, ps, 64, NH, wt, hT, hTb, 0)
                    kb.op("act", lambda e: e.copy(kidx_own.t[:, n0:n0 + NH], ps.t[:64, :NH]), reads=[ps.b], writes=[kidx_own.b])
                    for tq in range(4):
                        ps = nextps()

                        def mmw(e):
                            ins = None
                            for kc in range(NKC):
                                ins = e.matmul(ps.t[:, :32], hT.t[:, kc, tq * 128:(tq + 1) * 128], wt.t[:, kc, 64:96],
                                               start=(kc == 0), stop=(kc == NKC - 1))
                            return ins
                        kb.op("pe", mmw, reads=[wt.b] + hTb, writes=[ps.b], self_sync=False)
                        kb.op("act", lambda e: e.activation(out=w_tm.t[:, half * 4 + tq, :], in_=ps.t[:, :32], func=AF.Copy, scale=WI),
                              reads=[ps.b], writes=[w_tm.b])
                    kb.barrier()
                    rms_fm(kb, C, zq, zqb, 8, NH, P["gcq"], lambda fc: (cqT.t[:, fc, :], cqb[fc]), "cd_nq")
                    rms_fm(kb, C, zkv, zkvb, 4, NH, P["gckv"], lambda fc: (ckvT_own.t[:, fc, n0:n0 + NH], ckvTob[fc]), "cd_nkv")
                    w8 = [kb.sb(es1, "cd_w8%d" % i, [128, 8, 128], bf) for i in range(2)]
                    stg = [kb.sb(es1, "cd_stg%d" % i, [128, NH], bf) for i in range(3)]
                    for i in range(32):
                        wsrc, dst_d, hh = (w_uq, q_d, i) if i < 16 else (w_qidx, qi_d, i - 16)
                        wt8 = w8[i % 2]
                        kb.dma("sp", wt8.t[:], wsrc.t[:, hh * 128:(hh + 1) * 128].rearrange("(kc p) f -> p kc f", p=128),
                               reads=[wsrc.b], writes=[wt8.b])
                        ps = nextps()

                        def mm8(e):
                            ins = None
                            for kc in range(8):
                                ins = e.matmul(ps.t[:, :NH], wt8.t[:, kc, :], cqT.t[:, kc, :], start=(kc == 0), stop=(kc == 7))
                            return ins
                        kb.op("pe", mm8, reads=[wt8.b] + cqb, writes=[ps.b], self_sync=False)
                        sg_ = stg[i % 3]
                        if i % 2:
                            kb.op("act", lambda e: e.copy(sg_.t[:], ps.t[:, :NH]), reads=[ps.b], writes=[sg_.b])
                        else:
                            kb.op("dve", lambda e: e.tensor_copy(sg_.t[:], ps.t[:, :NH]), reads=[ps.b], writes=[sg_.b])
                        kb.dma("sp", dst_d.t[hh][:, n0:n0 + NH], sg_.t[:], reads=[sg_.b], writes=[dst_d.b])
                    kb.barrier()
                with ExitStack() as es1:
                    dT = kb.sb(es1, "cd_dT", [128, 16, NH], bf)
                    dTb = [Buf() for _ in range(16)]
                    yC = kb.sb(es1, "cd_yC", [128, 16, NH], bf)
                    yCb = [Buf() for _ in range(16)]
                    invc = kb.sb(es1, "cd_invc", [128, 4, NH], F32)
                    kb.dma("sp", invc.t[:], P["invcnt_d"].t[:, :].rearrange("p (g t) -> p g t", g=4)[:, :, n0:n0 + NH], writes=[invc.b])
                    ta = [kb.sb(es1, "cd_ta%d" % i, [128, HB + NH], F32) for i in range(2)]
                    tb = [kb.sb(es1, "cd_tb%d" % i, [128, HB + NH], F32) for i in range(2)]
                    LT = HB + NH
                    for fc in range(16):
                        gi = fc // 4
                        eng = "dve" if fc % 2 else "pool"
                        a_, b_ = ta[fc % 2], tb[fc % 2]
                        src = pT.t[:, fc, :]
                        srcb = pTb[fc]
                        cur, curb = src, srcb
                        sh = 1
                        bufs2 = [a_, b_]
                        for lv in range(gi + 1):
                            dstt = bufs2[lv % 2]
                            c_ap = cur
                            kb.op(eng, lambda e: e.tensor_tensor(out=dstt.t[:, 16:LT], in0=c_ap[:, 16:LT], in1=c_ap[:, 16 - sh:LT - sh], op=ALU.add),
                                  reads=[curb], writes=[dstt.b])
                            cur, curb = dstt.t[:], dstt.b
                            sh *= 2
                        m_ = bufs2[(gi + 1) % 2]
                        c_ap = cur
                        kb.op(eng, lambda e: e.tensor_tensor(out=m_.t[:, HB:LT], in0=c_ap[:, HB:LT], in1=invc.t[:, gi, :], op=ALU.mult),
                              reads=[curb, invc.b], writes=[m_.b])
                        kb.op(eng, lambda e: e.tensor_tensor(out=dT.t[:, fc, :], in0=m_.t[:, HB:LT], in1=src[:, HB:LT], op=ALU.subtract),
                              reads=[m_.b, srcb], writes=[dTb[fc]])
                        kb.op(eng, lambda e: e.tensor_copy(pT.t[:, fc, 0:HB], pT.t[:, fc, NH:NH + HB]), reads=[pTb[fc]], writes=[pTb[fc]])
                    for ec in range(16):
                        gi, el = ec // 4, ec % 4
                        ps = nextps()

                        def mmp(e):
                            ins = None
                            for cc in range(4):
                                ins = e.matmul(ps.t[:, :NH], wpool.t[:, gi * 4 + cc, el * 128:(el + 1) * 128], dT.t[:, gi * 4 + cc, :],
                                               start=(cc == 0), stop=(cc == 3))
                            return ins
                        kb.op("pe", mmp, reads=[wpool.b] + dTb[gi * 4:gi * 4 + 4], writes=[ps.b], self_sync=False)
                        kb.op("act", lambda e: e.activation(out=yC.t[:, ec, :], in_=ps.t[:, :NH], func=AF.Copy, scale=P["poolsc"].t[:, ec:ec + 1]),
                              reads=[ps.b, P["poolsc"].b], writes=[yCb[ec]])
                    out_proj(kb, C, w_out, lambda fc: (yC.t[:, fc, :], yCb[fc]), n0, NH, "cd_oc", nk=16, krow0=0)
            kb.barrier()

        with ExitStack() as es1:
            kb.dma("sp", ckvT_s.t[:, :].rearrange("(rc p) t -> p rc t", p=128), ckvT_own.t[:], reads=ckvTob, writes=[ckvT_s.b])
            kb.dma("sp", kidx_s.t[:, :], kidx_own.t[:], reads=[kidx_own.b], writes=[kidx_s.b])
            stg = [kb.sb(es1, "cd_tst%d" % i, [128, 512], bf) for i in range(2)]
            for tq in range(8):
                sg_ = stg[tq % 2]
                for rc in range(4):
                    ps = nextps()
                    kb.op("pe", lambda e: e.matmul(ps.t[:, :128], ckvT_own.t[:, rc, tq * 128:(tq + 1) * 128], identb.t[:], start=True, stop=True),
                          reads=[ckvTob[rc], identb.b], writes=[ps.b], self_sync=False)
                    kb.op("act" if rc % 2 else "dve",
                          (lambda e: e.copy(sg_.t[:, rc * 128:(rc + 1) * 128], ps.t[:, :128])) if rc % 2 else
                          (lambda e: e.tensor_copy(sg_.t[:, rc * 128:(rc + 1) * 128], ps.t[:, :128])),
                          reads=[ps.b], writes=[sg_.b])
                kb.dma("sp", ckvm_s.t[tq * 128:(tq + 1) * 128, :], sg_.t[:], reads=[sg_.b], writes=[ckvm_s.b])
            kb.cc("AllGather", G4, ckvT_s.t, ckvT_a.t, reads=[ckvT_s.b], writes=[ckvT_a.b])
            kb.cc("AllGather", G4, ckvm_s.t, ckvm_a.t, reads=[ckvm_s.b], writes=[ckvm_a.b])
            kb.cc("AllGather", G4, kidx_s.t, kidx_a.t, reads=[kidx_s.b], writes=[kidx_a.b])
            kb.barrier()

        def sel_combine(dst_fn, src_fn, ops_shape_ok=True):
            for sl in range(3):
                r = 3 - sl
                d_ap, d_b = dst_fn(sl)
                for q in range(4):
                    s_ap, s_b = src_fn(q)
                    sc = P["selr"].t[:, (r - 1) * 4 + q:(r - 1) * 4 + q + 1]
                    if q == 0:
                        kb.op("dve", lambda e: e.tensor_scalar(out=d_ap, in0=s_ap, scalar1=sc, scalar2=None, op0=ALU.mult),
                              reads=[s_b, P["selr"].b], writes=[d_b])
                    else:
                        kb.op("dve", lambda e: e.scalar_tensor_tensor(out=d_ap, in0=s_ap, scalar=sc, in1=d_ap, op0=ALU.mult, op1=ALU.add),
                              reads=[s_b, P["selr"].b, d_b], writes=[d_b])

        with ExitStack() as esK:
            maskT = kb.sb(esK, "cd_maskT", [128, 32, NH], bf)
            for hf in range(2):
                n0 = hf * NH
                b0 = 24 + 4 * hf
                JS = b0 + 4
                with ExitStack() as es1:
                    kidx_k = kb.sb(es1, "cd_kidxk", [128, 4, TOK], bf)
                    with ExitStack() as es2:
                        tmp = kb.sb(es2, "cd_kitmp", [128, 4, TOK], bf)
                        for hp in range(2):
                            kb.dma("sp", tmp.t[hp * 64:(hp + 1) * 64], kidx_a.t[:, :].rearrange("(q d) t -> d q t", q=4),
                                   reads=[kidx_a.b], writes=[tmp.b])
                            kb.dma("sp", kidx_k.t[hp * 64:(hp + 1) * 64, 3, :], kidx_s.t[:, :], reads=[kidx_s.b], writes=[kidx_k.b])
                        sel_combine(lambda sl: (kidx_k.t[:, sl, :], kidx_k.b), lambda q: (tmp.t[:, q, :], tmp.b))
                        kb.barrier()
                    kb.op("pool", lambda e: e.memset(maskT.t[:], 0.0), writes=[maskT.b])
                    acc = kb.sb(es1, "cd_acc", [128, 32 * 128], F32)
                    work = kb.sb(es1, "cd_work", [128, 32 * 128], F32)
                    mkf = kb.sb(es1, "cd_mkf", [128, 32 * 128], F32)
                    qi = [kb.sb(es1, "cd_qi%d" % i, [128, 16, 128], bf) for i in range(2)]
                    rl = [kb.sb(es1, "cd_rl%d" % i, [128, 512], F32) for i in range(3)]
                    m8 = kb.sb(es1, "cd_m8", [128, 8], F32)
                    thr = kb.sb(es1, "cd_thr", [128, 1], F32)
                    kk = kidx_k.t[:].rearrange("p s t -> p (s t)")
                    nr = 0
                    for ib in range(4):
                        il = hf * 4 + ib
                        pos = 24 + il
                        S = (pos + 1) * 128
                        qt = qi[ib % 2]
                        kb.dma("sp", qt.t[:], qi_d.t[:, :, il * 128:(il + 1) * 128].rearrange("f p t -> p f t"), reads=[qi_d.b], writes=[qt.b])
                        for hi in range(32):
                            fc, po = hi // 2, (hi % 2) * 64
                            for sc_ in range((S + 511) // 512):
                                n = min(512, S - sc_ * 512)
                                ps = nextps()
                                kb.op("pe", lambda e: e.matmul(ps.t[:, :n], qt.t[po:po + 64, fc, :], kk[po:po + 64, sc_ * 512:sc_ * 512 + n], start=True, stop=True),
                                      reads=[qt.b, kidx_k.b], writes=[ps.b], self_sync=False)
                                r_ = rl[nr % 3]
                                nr += 1
                                kb.op("act", lambda e: e.activation(out=r_.t[:, :n], in_=ps.t[:, :n], func=AF.Relu), reads=[ps.b], writes=[r_.b])
                                a_ap = acc.t[:, sc_ * 512:sc_ * 512 + n]
                                wsc = w_tm.t[:, il, hi:hi + 1]
                                if hi == 0:
                                    kb.op("dve", lambda e: e.tensor_scalar(out=a_ap, in0=r_.t[:, :n], scalar1=wsc, scalar2=None, op0=ALU.mult),
                                          reads=[r_.b, w_tm.b], writes=[acc.b])
                                else:
                                    kb.op("dve", lambda e: e.scalar_tensor_tensor(out=a_ap, in0=r_.t[:, :n], scalar=wsc, in1=a_ap, op0=ALU.mult, op1=ALU.add),
                                          reads=[r_.b, w_tm.b, acc.b], writes=[acc.b])
                        for sl in range(3):
                            kb.op("dve", lambda e: e.tensor_scalar(out=acc.t[:, sl * TOK:(sl + 1) * TOK], in0=acc.t[:, sl * TOK:(sl + 1) * TOK],
                                                                   scalar1=P["negslot"].t[:, sl:sl + 1], scalar2=None, op0=ALU.add),
                                  reads=[acc.b, P["negslot"].b], writes=[acc.b])
                        kb.op("dve", lambda e: e.tensor_tensor(out=acc.t[:, pos * 128:S], in0=acc.t[:, pos * 128:S], in1=P["negtri"].t[:], op=ALU.add),
                              reads=[acc.b, P["negtri"].b], writes=[acc.b])
                        kb.op("pool", lambda e: e.tensor_copy(work.t[:, :S], acc.t[:, :S]), reads=[acc.b], writes=[work.b])
                        for rd in range(32):
                            kb.op("dve", lambda e: e.max(out=m8.t[:], in_=work.t[:, :S]), reads=[work.b], writes=[m8.b])
                            if rd < 31:
                                kb.op("dve", lambda e: e.match_replace(out=work.t[:, :S], in_to_replace=m8.t[:], in_values=work.t[:, :S], imm_value=NEG),
                                      reads=[m8.b, work.b], writes=[work.b])
                        kb.op("dve", lambda e: e.tensor_reduce(out=thr.t[:], in_=m8.t[:], axis=AX.X, op=ALU.min), reads=[m8.b], writes=[thr.b])
                        kb.op("dve", lambda e: e.tensor_scalar(out=mkf.t[:, :S], in0=acc.t[:, :S], scalar1=thr.t[:, 0:1], scalar2=None, op0=ALU.is_ge),
                              reads=[acc.b, thr.b], writes=[mkf.b])
                        for sl in range(3):
                            kb.op("pool", lambda e: e.tensor_scalar(out=mkf.t[:, sl * TOK:(sl + 1) * TOK], in0=mkf.t[:, sl * TOK:(sl + 1) * TOK],
                                                                    scalar1=P["valid01"].t[:, sl:sl + 1], scalar2=None, op0=ALU.mult),
                                  reads=[mkf.b, P["valid01"].b], writes=[mkf.b])
                        kb.op("pool", lambda e: e.tensor_tensor(out=mkf.t[:, pos * 128:S], in0=mkf.t[:, pos * 128:S], in1=P["tril"].t[:], op=ALU.mult),
                              reads=[mkf.b, P["tril"].b], writes=[mkf.b])
                        for j4 in range((pos + 4) // 4):
                            ps = nextps()
                            nj = min(4, pos + 1 - j4 * 4)

                            def mmt(e):
                                ins = None
                                for jj in range(nj):
                                    js = j4 * 4 + jj
                                    ins = e.transpose(ps.t[:, jj * 128:(jj + 1) * 128], mkf.t[:, js * 128:(js + 1) * 128], C.ident.t[:])
                                return ins
                            kb.op("pe", mmt, reads=[mkf.b, C.ident.b], writes=[ps.b], self_sync=False)
                            o = maskT.t[:, j4 * 4:j4 * 4 + nj, ib * 128:(ib + 1) * 128]
                            i_ = ps.t[:, :nj * 128].rearrange("p (j t) -> p j t", j=nj)
                            kb.op("act", lambda e: e.copy(o, i_), reads=[ps.b], writes=[maskT.b])
                    kb.barrier()
                yD = kb.sb(esK, "cd_yD", [128, 16, NH], bf)
                yDb = [Buf() for _ in range(16)]
                with ExitStack() as es1:
                    ckvT_k = kb.sb(es1, "cd_ckvTk", [128, 4, 4, TOK], bf)
                    ckvm_k = kb.sb(es1, "cd_ckvmk", [128, 4, 8, 512], bf)
                    with ExitStack() as es2:
                        tmp = kb.sb(es2, "cd_kvtmp", [128, 4, 4, TOK], bf)
                        for q in range(4):
                            kb.dma("sp", tmp.t[:, q], ckvT_a.t[q * 512:(q + 1) * 512, :].rearrange("(rc p) t -> p rc t", p=128),
                                   reads=[ckvT_a.b], writes=[tmp.b])
                        kb.op("pool", lambda e: e.tensor_copy(ckvT_k.t[:, :, 3, :], ckvT_own.t[:]), reads=ckvTob, writes=[ckvT_k.b])
                        sel_combine(lambda sl: (ckvT_k.t[:, :, sl, :], ckvT_k.b), lambda q: (tmp.t[:, q], tmp.b))
                        kb.barrier()
                    with ExitStack() as es2:
                        tmp = kb.sb(es2, "cd_kmtmp", [128, 4, 8, 512], bf)
                        for q in range(4):
                            kb.dma("sp", tmp.t[:, q], ckvm_a.t[q * TOK:(q + 1) * TOK, :].rearrange("(c p) r -> p c r", p=128),
                                   reads=[ckvm_a.b], writes=[tmp.b])
                        kb.dma("sp", ckvm_k.t[:, 3], ckvm_s.t[:, :].rearrange("(c p) r -> p c r", p=128), reads=[ckvm_s.b], writes=[ckvm_k.b])
                        sel_combine(lambda sl: (ckvm_k.t[:, sl], ckvm_k.b), lambda q: (tmp.t[:, q], tmp.b))
                        kb.barrier()
                    qh = [kb.sb(es1, "cd_qh%d" % i, [128, NH], bf) for i in range(2)]
                    wuk = [kb.sb(es1, "cd_wuk%d" % i, [128, 512], bf) for i in range(2)]
                    wuv = [kb.sb(es1, "cd_wuv%d" % i, [128, 4, 128], bf) for i in range(2)]
                    btt = [kb.sb(es1, "cd_btt%d" % i, [128, 5, NH], bf) for i in range(1)]
                    qlat = [kb.sb(es1, "cd_qlat%d" % i, [128, 4, NH], bf) for i in range(2)]
                    Eb = [kb.sb(es1, "cd_E%d" % i, [128, NH], bf) for i in range(3)]
                    pmb = [kb.sb(es1, "cd_pm%d" % i, [128, NH], bf) for i in range(3)]
                    rden = kb.sb(es1, "cd_rden", [128, NH], F32)
                    olat = kb.sb(es1, "cd_olat", [128, 4, NH], bf)
                    accb = [C.ps[i] for i in range(4)]
                    denb = C.ps[4]
                    lgb = [C.ps[5], C.ps[6]]
                    xb_ = C.ps[7]
                    ne = 0
                    for h in range(16):
                        kb.bg(2)
                        q_, uk_, uv_, bt_, ql_ = qh[h % 2], wuk[h % 2], wuv[h % 2], btt[0], qlat[h % 2]
                        kb.dma("sp", q_.t[:], q_d.t[h][:, n0:n0 + NH], reads=[q_d.b], writes=[q_.b])
                        kb.dma("sp", uk_.t[:], w_uk.t[h * 128:(h + 1) * 128, :], reads=[w_uk.b], writes=[uk_.b])
                        kb.dma("sp", uv_.t[:], w_uv.t[h * 512:(h + 1) * 512, :].rearrange("(rc p) d -> p rc d", p=128), reads=[w_uv.b], writes=[uv_.b])
                        kb.dma("sp", bt_.t[:], bass.AP(RN_t, h * RNK + 1, [[1, 128], [128, 5], [1, NH]]), reads=[RN_d.b], writes=[bt_.b])
                        for rc in range(4):
                            ps = lgb[rc % 2]
                            kb.op("pe", lambda e: e.matmul(ps.t[:, :NH], uk_.t[:, rc * 128:(rc + 1) * 128], q_.t[:], start=True, stop=True),
                                  reads=[uk_.b, q_.b], writes=[ps.b], self_sync=False)
                            kb.op("act", lambda e: e.copy(ql_.t[:, rc, :], ps.t[:, :NH]), reads=[ps.b], writes=[ql_.b])
                        for js in range(JS):
                            sl, ch = js // 8, js % 8
                            dj = js - b0
                            near = dj >= -1
                            ps = lgb[js % 2]

                            def mml(e):
                                ins = None
                                for rc in range(4):
                                    ins = e.matmul(ps.t[:, :NH], ckvT_k.t[:, rc, sl, ch * 128:(ch + 1) * 128], ql_.t[:, rc, :],
                                                   start=(rc == 0), stop=(rc == 3 and not near))
                                if near:
                                    ins = e.matmul(ps.t[:, :NH], Jb.t[:], bt_.t[:, 3 - dj, :], start=False, stop=True)
                                return ins
                            kb.op("pe", mml, reads=[ckvT_k.b, ql_.b, Jb.b, bt_.b], writes=[ps.b], self_sync=False)
                            E_ = Eb[ne % 3]
                            pm_ = pmb[ne % 3]
                            ne += 1
                            if near:
                                kb.op("act", lambda e: e.activation(out=E_.t[:], in_=ps.t[:, :NH], func=AF.Exp, scale=SC), reads=[ps.b], writes=[E_.b])
                            else:
                                kb.op("act", lambda e: e.activation(out=E_.t[:], in_=ps.t[:, :NH], func=AF.Exp, scale=SC, bias=P["rb31"].t[:, h:h + 1]),
                                      reads=[ps.b, P["rb31"].b], writes=[E_.b])
                            kb.op("dve" if ne % 2 else "pool", lambda e: e.tensor_tensor(out=pm_.t[:], in0=E_.t[:], in1=maskT.t[:, js, :], op=ALU.mult),
                                  reads=[E_.b, maskT.b], writes=[pm_.b])

                            def mmv(e):
                                ins = None
                                for rc in range(4):
                                    ins = e.matmul(accb[rc].t[:, :NH], ckvm_k.t[:, sl, ch, rc * 128:(rc + 1) * 128], pm_.t[:],
                                                   start=(js == 0), stop=(js == JS - 1))
                                ins = e.matmul(denb.t[:, :NH], onesb.t[:], pm_.t[:], start=(js == 0), stop=(js == JS - 1))
                                return ins
                            kb.op("pe", mmv, reads=[ckvm_k.b, pm_.b, onesb.b], writes=[a.b for a in accb] + [denb.b], self_sync=False)
                        kb.op("dve", lambda e: e.reciprocal(out=rden.t[:], in_=denb.t[:, :NH]), reads=[denb.b], writes=[rden.b])
                        for rc in range(4):
                            kb.op("dve", lambda e: e.tensor_tensor(out=olat.t[:, rc, :], in0=accb[rc].t[:, :NH], in1=rden.t[:], op=ALU.mult),
                                  reads=[accb[rc].b, rden.b], writes=[olat.b])

                        def mmo(e):
                            ins = None
                            for rc in range(4):
                                ins = e.matmul(xb_.t[:, :NH], uv_.t[:, rc, :], olat.t[:, rc, :], start=(rc == 0), stop=(rc == 3))
                            return ins
                        kb.op("pe", mmo, reads=[uv_.b, olat.b], writes=[xb_.b], self_sync=False)
                        kb.op("act", lambda e: e.copy(yD.t[:, h, :], xb_.t[:, :NH]), reads=[xb_.b], writes=[yDb[h]])
                    kb.barrier()
                out_proj(kb, C, w_out, lambda fc: (yD.t[:, fc, :], yDb[fc]), n0, NH, "cd_od", nk=16, krow0=16)
            kb.barrier()
        kb.barrier()

SMALL = {
    "ident": ([128, 128], None),
    "g_ff": ([128, 4 * NKC], None),
    "g_final": ([128, NKC], None),
    "g_mix": ([128, 2 * NKC], ("ab", "cd")),
    "sel": ([128, 4], ("ab", "cd")),
    "lnag": ([128, 16], ("ab",)), "lnab": ([128, 16], ("ab",)), "lnbg": ([128, 16], ("ab",)), "lnbb": ([128, 16], ("ab",)),
    "convw": ([128, 16, KW], ("ab",)), "convb": ([128, 16], ("ab",)),
    "wspT": ([128, 8, 128], ("ab",)), "tri": ([128, 128], ("ab",)), "bspbc": ([128, 8, 128], ("ab",)),
    "poolsc": ([128, 16], ("cd",)), "gcq": ([128, 8], ("cd",)), "gckv": ([128, 4], ("cd",)), "selr": ([128, 12], ("cd",)),
    "negslot": ([128, 3], ("cd",)), "valid01": ([128, 3], ("cd",)), "negtri": ([128, 128], ("cd",)), "tril": ([128, 128], ("cd",)),
    "Jf": ([128, 128], ("cd",)), "rb": ([128, 16], ("cd",)), "OHk": ([128, RNK], ("cd",)), "rb31": ([128, 16], ("cd",)),
}


def small_needed(stages):
    return [k for k, (shp, st) in SMALL.items() if st is None or any(x in stages for x in st)]


def build_program(stages, dump=False):
    nc = bass.Bass("TRN2", target_bir_lowering=False)
    es = ExitStack()
    with es:
        kb = KB(nc, es)
        C = Ctx()
        x_in = nc.dram_tensor("x", [TOK, D], F32, kind="ExternalInput").ap()
        out_ap = nc.dram_tensor("out", [TOK, D], F32, kind="ExternalOutput").ap()
        xT_d = nc.dram_tensor("xT_res", [NKC, 128, TOK], F32).ap()
        C.xTb = [[Buf(), Buf()] for _ in range(NKC)]
        C.outb = Buf()
        C.xT = xT_d
        P = {}
        sm_in = {}
        for name in small_needed(stages):
            shp = SMALL[name][0]
            flat = [128, int(np.prod(shp[1:]))]
            sm_in[name] = nc.dram_tensor(name, flat, F32, kind="ExternalInput").ap()
            P[name] = kb.sb(es, name + "_sb", shp, F32)

        W = {}
        seq = []
        for st in stages:
            if st.startswith("ffn"):
                l, j = int(st[3]), int(st[4])
                wi = make_weight(kb, nc, "w_ff_in_%d%d" % (l, j), 2 * DFF, D)
                wo = make_weight(kb, nc, "w_ff_out_%d%d" % (l, j), DFF, D)
                W[st] = (wi, wo)
                f0 = 0
                for G in FFN_GROUPS:
                    seq += wi.pieces_for_rows(2 * f0 * 128, 2 * (f0 + G) * 128)
                    seq += wo.pieces_for_rows(f0 * 128, (f0 + G) * 128)
                    f0 += G
            elif st == "ab":
                W[st] = (make_weight(kb, nc, "w_in_ab", D, 8192), make_weight(kb, nc, "w_out_ab", D, D))
                for w in W[st]:
                    seq += w.all_pieces()
            elif st == "cd":
                W[st] = (make_weight(kb, nc, "w_in_cd", D, 3680), make_weight(kb, nc, "w_out_cd", D, D),
                         make_weight(kb, nc, "w_pool", 2048, 512), make_weight(kb, nc, "w_uq", 1024, 2048),
                         make_weight(kb, nc, "w_qidx", 1024, 2048), make_weight(kb, nc, "w_uk", 2048, 512),
                         make_weight(kb, nc, "w_uv", 8192, 128))
                P["invcnt_d"] = T(nc.dram_tensor("invcnt", [128, 4 * TOK], F32, kind="ExternalInput").ap())
                for w in (W[st][0], W[st][2], W[st][3], W[st][4], W[st][5], W[st][6], W[st][1]):
                    seq += w.all_pieces()
        enqueue_pieces(kb, seq)

        C.ps = [T(es.enter_context(nc.psum_tensor("ps%d" % i, [128, 512], F32)), Buf()) for i in range(8)]
        C.ident = P["ident"]
        C.ones = kb.sb(es, "ones_sb", [128, 128], F32)
        C.gff = P["g_ff"]
        C.gfin = P["g_final"]
        if "sel" in P:
            C.sel = P["sel"]
        if "g_mix" in P:
            P["gmix0"] = T(P["g_mix"].t[:, 0:NKC], P["g_mix"].b)
            P["gmix1"] = T(P["g_mix"].t[:, NKC:2 * NKC], P["g_mix"].b)

        block = es.enter_context(nc.Block())

        @block.gpsimd
        def _(g):
            for name, src in sm_in.items():
                t = P[name]
                shp = SMALL[name][0]
                if len(shp) == 2:
                    dst = t.t[:]
                else:
                    dst = t.t[:].rearrange("p a b -> p (a b)")
                kb.dma("sp", dst, src[:, :], writes=[t.b])
            kb.op("dve", lambda e: e.memset(C.ones.t[:], 1.0), writes=[C.ones.b])
            kb.bg(12)
            load_x(kb, C, x_in)
            for si, st in enumerate(stages):
                if st.startswith("ffn"):
                    l, j = int(st[3]), int(st[4])
                    gi = l * 2 + j
                    gT = T(C.gff.t[:, gi * NKC:(gi + 1) * NKC], C.gff.b)
                    ffn_stage(kb, C, gT, W[st][0], W[st][1], bg_per_group=2)
                elif st == "ab":
                    for w in W[st]:
                        w.need_all(kb)
                    mixer_ab(kb, C, nc, P, W[st][0], W[st][1])
                elif st == "cd":
                    for w in W[st]:
                        w.need_all(kb)
                    mixer_cd(kb, C, nc, P, W[st])
            kb.bg(100000)
            final_norm_store(kb, C, C.gfin, out_ap, do_norm=not dump)
            kb.barrier()
    return nc


ALL_STAGES = ["ffn00", "ab", "ffn01", "ffn10", "cd", "ffn11"]


def lay128(v):
    v = np.asarray(v, np.float32)
    return np.ascontiguousarray(v.reshape(-1, 128).T)


def small_values(inputs, c):
    f = np.float32
    v = {}
    v["ident"] = np.eye(128, dtype=f)
    v["g_ff"] = np.concatenate([lay128(inputs["g_ff"][l, j]) for l in range(2) for j in range(2)], axis=1)
    v["g_final"] = lay128(inputs["g_final"])
    v["g_mix"] = np.concatenate([lay128(inputs["g_mix"][0]), lay128(inputs["g_mix"][1])], axis=1)
    sel = np.zeros((128, 4), f)
    if c % 4 > 0:
        sel[:, c % 4 - 1] = 1.0
    v["sel"] = sel
    v["lnag"] = lay128(inputs["ln_a_g"][0])
    v["lnab"] = lay128(inputs["ln_a_b"][0])
    v["lnbg"] = lay128(inputs["ln_b_g"][0])
    v["lnbb"] = lay128(inputs["ln_b_b"][0])
    cw = np.asarray(inputs["conv_w"][0], f)
    v["convw"] = cw.T.reshape(16, 128, KW).transpose(1, 0, 2)
    v["convb"] = lay128(inputs["conv_b"][0])
    v["wspT"] = np.asarray(inputs["w_sp"][0], f).transpose(2, 0, 1)
    jj = np.arange(128)
    v["tri"] = (jj[:, None] <= jj[None, :]).astype(f)
    v["bspbc"] = np.broadcast_to(np.asarray(inputs["b_sp"][0], f)[None], (128, 8, 128))
    cq = c % 4
    v["poolsc"] = lay128(inputs["pool_scale"][0])
    v["gcq"] = lay128(inputs["g_cq"][0])
    v["gckv"] = lay128(inputs["g_ckv"][0])
    selr = np.zeros((128, 3, 4), f)
    neg = np.zeros((128, 3), f)
    val = np.zeros((128, 3), f)
    for r in (1, 2, 3):
        if cq - r >= 0:
            selr[:, r - 1, cq - r] = 1.0
    for sl in range(3):
        r = 3 - sl
        ok = cq - r >= 0
        neg[:, sl] = 0.0 if ok else NEG
        val[:, sl] = 1.0 if ok else 0.0
    v["selr"] = selr
    v["negslot"] = neg
    v["valid01"] = val
    tt = np.arange(128)
    v["negtri"] = np.where(tt[None, :] <= tt[:, None], 0.0, NEG).astype(f)
    v["tril"] = (tt[None, :] <= tt[:, None]).astype(f)
    v["Jf"] = np.eye(128, dtype=f)[::-1].copy()
    rb = np.zeros((128, 16), f)
    rb[:32] = np.asarray(inputs["rel_bias"], f)
    v["rb"] = rb
    n = np.arange(RNK) - RNPAD
    nf = np.maximum(n, 1).astype(f)
    large = 16 + (np.log(nf / f(16)) / f(np.log(128 / 16)) * f(16)).astype(np.int32)
    large = np.minimum(large, 31)
    bucket = np.where(n < 16, np.maximum(n, 0), large)
    oh = np.zeros((128, RNK), f)
    for k in range(RNK):
        if n[k] >= 0:
            oh[bucket[k], k] = 1.0
    v["OHk"] = oh
    v["rb31"] = np.broadcast_to(np.asarray(inputs["rel_bias"], f)[31][None], (128, 16))
    pos = cq * TOK + np.arange(TOK)
    inv = np.stack([1.0 / np.minimum(pos + 1, w) for w in (2, 4, 8, 16)], 0).astype(f)
    v["invcnt"] = np.broadcast_to(inv.reshape(1, 4 * TOK), (128, 4 * TOK))
    return v


def tile_w_in(w):
    w = np.asarray(w, np.float32)
    return np.ascontiguousarray(w.reshape(NKC, 128, 2, NFC, 128).transpose(3, 2, 1, 0, 4)).reshape(2 * DFF, D)


def make_in_maps(inputs, stages):
    x = np.asarray(inputs["x"], np.float32).reshape(8 * TOK, D)
    names = small_needed(stages)
    maps = []
    for c in range(8):
        m = {"x": x[c * TOK:(c + 1) * TOK]}
        sv = small_values(inputs, c)
        for n in names:
            m[n] = np.ascontiguousarray(np.asarray(sv[n], np.float32).reshape(128, -1))
        if "cd" in stages:
            m["invcnt"] = np.ascontiguousarray(sv["invcnt"], dtype=np.float32)
        maps.append(m)

    def put(name, w):
        w = np.asarray(w, np.float32)
        for c in range(8):
            maps[c][name] = shard_weight(w, c)

    for st in stages:
        if st.startswith("ffn"):
            l, j = int(st[3]), int(st[4])
            put("w_ff_in_%d%d" % (l, j), tile_w_in(inputs["w_ff_in"][l, j]))
            put("w_ff_out_%d%d" % (l, j), inputs["w_ff_out"][l, j])
        elif st == "ab":
            put("w_in_ab", inputs["w_in_ab"][0])
            put("w_out_ab", inputs["w_out_ab"][0])
        elif st == "cd":
            put("w_in_cd", inputs["w_in_cd"][0])
            put("w_out_cd", inputs["w_out_cd"][0])
            put("w_pool", np.asarray(inputs["w_pool"][0]).reshape(2048, 512))
            put("w_uq", inputs["w_uq"][0])
            put("w_qidx", inputs["w_qidx"][0])
            put("w_uk", np.asarray(inputs["w_uk"][0]).reshape(2048, 512))
            put("w_uv", np.asarray(inputs["w_uv"][0]).reshape(8192, 128))
    return maps


def run(inputs, stages, dump=False):
    nc = build_program(stages, dump=dump)
    maps = make_in_maps(inputs, stages)
    res = run_bass_kernel_spmd(nc, maps, core_ids=list(range(8)))
    out = np.concatenate([res.results[c]["out"] for c in range(8)], axis=0)
    return out.reshape(2, 4096, D)


def kernel(**inputs):
    return run(inputs, ALL_STAGES, dump=False)
```

```python
import numpy as np
from contextlib import ExitStack
import concourse.bass as bass
import concourse.mybir as mybir
from concourse.bass_utils import run_bass_kernel_spmd

F32 = mybir.dt.float32
BF16 = mybir.dt.bfloat16
AF = mybir.ActivationFunctionType
ALU = mybir.AluOpType
AX = mybir.AxisListType

D = 4096
NKC = 32
TOK = 1024
DFF = 11008
NFC = 86
EPS = 1e-6
G4 = [[0, 1, 2, 3], [4, 5, 6, 7]]
G2 = [[0, 4], [1, 5], [2, 6], [3, 7]]


class Buf:
    __slots__ = ("name", "w", "r")

    def __init__(self, name=""):
        self.name = name
        self.w = None
        self.r = {}


class T:
    __slots__ = ("t", "b")

    def __init__(self, t, b=None):
        self.t = t
        self.b = b if b is not None else Buf()


class KB:
    ENG = ("pe", "act", "dve", "pool", "sp")

    def __init__(self, nc, es):
        self.nc = nc
        self.es = es
        self.E = {"pe": nc.tensor, "act": nc.scalar, "dve": nc.vector, "pool": nc.gpsimd, "sp": nc.sync}
        self.S = {}
        self.cnt = {}
        for e in self.ENG:
            self.S[e] = es.enter_context(nc.semaphore("s_" + e))
            self.cnt[e] = 0
        self.ND = 4
        self.dq = {}
        for q in ("sp", "act", "pool"):
            self.dq[q] = 0
            for i in range(self.ND):
                key = "d_%s%d" % (q, i)
                self.S[key] = es.enter_context(nc.semaphore(key))
                self.cnt[key] = 0
        self.ccn = {"A": 4, "B": 8, "X": 1}
        self.cci = {"A": 0, "B": 0, "X": 0}
        for pool_, n in self.ccn.items():
            for i in range(n):
                key = "cc%s%d" % (pool_, i)
                self.S[key] = es.enter_context(nc.semaphore(key))
                self.cnt[key] = 0
        self.seen = {e: {} for e in self.ENG}
        self.bgq = []

    def sb(self, es, name, shape, dt):
        self.uid = getattr(self, "uid", 0) + 1
        name = "%s_%d" % (name, self.uid)
        return T(es.enter_context(self.nc.sbuf_tensor(name, list(shape), dt)), Buf(name))

    def _waits(self, e, reads, writes, self_sync=True):
        need = {}

        def add(dep):
            if dep is None:
                return
            key, val = dep
            if key == e and not self_sync:
                return
            if key.startswith("cc"):
                val = self.cnt[key]
            if need.get(key, 0) < val:
                need[key] = val

        for b in reads:
            add(b.w)
        for b in writes:
            add(b.w)
            for d in b.r.items():
                add(d)
        eng = self.E[e]
        sn = self.seen[e]
        for key, val in need.items():
            if sn.get(key, 0) >= val:
                continue
            eng.wait_ge(self.S[key], val)
            sn[key] = val

    def _mark(self, me, reads, writes):
        key, val = me
        for b in reads:
            if b.r.get(key, 0) < val:
                b.r[key] = val
        for b in writes:
            b.w = me
            b.r = {}

    def op(self, e, build, reads=(), writes=(), self_sync=True):
        self._waits(e, reads, writes, self_sync)
        ins = build(self.E[e])
        self.cnt[e] += 1
        ins.then_inc(self.S[e], 1)
        self._mark((e, self.cnt[e]), reads, writes)
        return ins

    def dma(self, q, out, in_, reads=(), writes=(), **kw):
        self._waits(q, reads, writes)
        key = "d_%s%d" % (q, self.dq[q] % self.ND)
        self.dq[q] += 1
        self.cnt[key] += 16
        self.E[q].dma_start(out=out, in_=in_, **kw).then_inc(self.S[key], 16)
        self._mark((key, self.cnt[key]), reads, writes)

    def cc(self, kind, groups, in_ap, out_ap, reads=(), writes=(), pool="X"):
        self._waits("pool", reads, writes)
        key = "cc%s%d" % (pool, self.cci[pool] % self.ccn[pool])
        self.cci[pool] += 1
        self.cnt[key] += 1
        self.nc.gpsimd.collective_compute(kind, ALU.bypass, replica_groups=groups,
                                          ins=[in_ap], outs=[out_ap]).then_inc(self.S[key], 1)
        self._mark((key, self.cnt[key]), reads, writes)

    def barrier(self):
        for e in self.ENG:
            sn = self.seen[e]
            for key, val in self.cnt.items():
                if key == e or val == 0:
                    continue
                if sn.get(key, 0) >= val:
                    continue
                self.E[e].wait_ge(self.S[key], val)
                sn[key] = val

    def bg(self, n=1):
        for _ in range(n):
            if not self.bgq:
                return
            key, fn = self.bgq.pop(0)
            fn()

    def bg_flush_to(self, key):
        last = -1
        for i, (k, _) in enumerate(self.bgq):
            if k == key:
                last = i
        for _ in range(last + 1):
            k, fn = self.bgq.pop(0)
            fn()


class Ctx:
    pass


def rows_split(n, k):
    base = n // k
    return [(i * base, base) for i in range(k)]


def piece_rows(rows, cols):
    rs = rows // 8
    best = 1
    for m in range(1, rs + 1):
        if rs % m == 0 and m * cols * 2 * 8 <= (4 << 20):
            best = m
    return best


def shard_weight(w, c):
    rows, cols = w.shape
    m = piece_rows(rows, cols)
    P = rows // (8 * m)
    return np.ascontiguousarray(w.reshape(P, 8, m, cols)[:, c].reshape(P * m, cols))


class Wt:
    def __init__(self, t, pb, rpp, name):
        self.t = t
        self.pb = pb
        self.rpp = rpp
        self.name = name
        self.b = pb[-1]
        self.steps = None
        self.ready_key = {}
        self.ready_buf = {}

    def need_rows(self, kb, r0, r1):
        p0, p1 = r0 // self.rpp, (r1 - 1) // self.rpp
        kb.bg_flush_to(self.ready_key[p1])
        return self.pb[p0:p1 + 1] + [self.ready_buf[p1]]

    def need_all(self, kb):
        pl = len(self.pb) - 1
        kb.bg_flush_to(self.ready_key[pl])
        self.b = self.ready_buf[pl]
        return self.pb

    def pieces_for_rows(self, r0, r1):
        return [(self, p) for p in range(r0 // self.rpp, (r1 - 1) // self.rpp + 1)]

    def all_pieces(self):
        return [(self, p) for p in range(len(self.pb))]


def make_weight(kb, nc, name, rows, cols, tag=None):
    rs = rows // 8
    m = piece_rows(rows, cols)
    P = rs // m
    src = nc.dram_tensor(name, [rs, cols], F32, kind="ExternalInput").ap()
    wb = nc.dram_tensor(name + "_b", [rs, cols], BF16).ap()
    wh = nc.dram_tensor(name + "_h", [P * 4 * m, cols], BF16).ap()
    wf = nc.dram_tensor(name + "_f", [rows, cols], BF16).ap()
    pb = [Buf() for _ in range(P)]
    w = Wt(wf, pb, 8 * m, name)
    steps = []
    for p in range(P):
        bb, bh = Buf(), Buf()

        def f1(p=p, bb=bb):
            kb.dma("pool", wb[p * m:(p + 1) * m, :], src[p * m:(p + 1) * m, :], reads=[], writes=[bb])

        def f2(p=p, bb=bb, bh=bh):
            kb.cc("AllGather", G4, wb[p * m:(p + 1) * m, :], wh[p * 4 * m:(p + 1) * 4 * m, :], reads=[bb], writes=[bh], pool="A")

        def f3(p=p, bh=bh):
            kb.cc("AllGather", G2, wh[p * 4 * m:(p + 1) * 4 * m, :], wf[p * 8 * m:(p + 1) * 8 * m, :], reads=[bh], writes=[pb[p]], pool="B")
        steps.append((f1, f2, f3))
    w.steps = steps
    return w


def enqueue_pieces(kb, seq):
    seen = set()
    order = []
    for w, p in seq:
        if (w.name, p) in seen:
            continue
        seen.add((w.name, p))
        order.append((w, p))
    n = len(order)
    for i, (w, p) in enumerate(order):
        w.ready_key[p] = (order[min(i + 3, n - 1)][0].name, order[min(i + 3, n - 1)][1])
        w.ready_buf[p] = order[min(i + 3, n - 1)][0].pb[order[min(i + 3, n - 1)][1]]
    for i in range(n + 2):
        if i < n:
            w, p = order[i]
            kb.bgq.append((None, w.steps[p][0]))
        if 0 <= i - 1 < n:
            w, p = order[i - 1]
            kb.bgq.append((None, w.steps[p][1]))
        if 0 <= i - 2 < n:
            w, p = order[i - 2]
            kb.bgq.append(((w.name, p), w.steps[p][2]))


def load_x(kb, C, x_in):
    nc = kb.nc
    with ExitStack() as es:
        xt = [kb.sb(es, "lx_xt%d" % i, [128, D], F32) for i in range(2)]
        st = [kb.sb(es, "lx_st%d" % i, [128, NKC, 128], F32) for i in range(2)]
        n = 0
        for tt in range(TOK // 128):
            a = xt[tt % 2]
            kb.dma("sp", a.t[:], x_in[tt * 128:(tt + 1) * 128, :], writes=[a.b])
            s = st[tt % 2]
            for k4 in range(NKC // 4):
                ps = C.ps[n % 8]
                n += 1

                def mm(e):
                    ins = None
                    for j in range(4):
                        kc = k4 * 4 + j
                        ins = e.transpose(ps.t[:, j * 128:(j + 1) * 128], a.t[:, kc * 128:(kc + 1) * 128], C.ident.t[:])
                    return ins
                kb.op("pe", mm, reads=[a.b, C.ident.b], writes=[ps.b], self_sync=False)
                eng = "act" if k4 % 2 else "dve"
                o = s.t[:, k4 * 4:(k4 + 1) * 4, :]
                i_ = ps.t[:].rearrange("p (j t) -> p j t", j=4)
                if eng == "act":
                    kb.op("act", lambda e: e.copy(o, i_), reads=[ps.b], writes=[s.b])
                else:
                    kb.op("dve", lambda e: e.tensor_copy(o, i_), reads=[ps.b], writes=[s.b])
            kb.dma("sp", C.xT[:, :, tt * 128:(tt + 1) * 128].rearrange("k p t -> p k t"), s.t[:],
                   reads=[s.b], writes=[b for kc in range(NKC) for b in C.xTb[kc]])
        kb.barrier()


def final_norm_store(kb, C, gT, out_ap, do_norm=True):
    with ExitStack() as es:
        rstd = kb.sb(es, "fn_rstd", [128, TOK], F32)
        if do_norm:
            rms_stats(kb, C, es, rstd)
        xc = [kb.sb(es, "fn_xc%d" % i, [128, TOK], F32) for i in range(3)]
        xn = [kb.sb(es, "fn_xn%d" % i, [128, TOK], F32) for i in range(2)]
        st = [kb.sb(es, "fn_st%d" % i, [128, 8, 512], F32) for i in range(2)]
        n = 0
        for k4 in range(NKC // 4):
            s = st[k4 % 2]
            for j in range(4):
                kc = k4 * 4 + j
                a = xc[kc % 3]
                kb.dma("sp", a.t[:], C.xT[kc], reads=C.xTb[kc], writes=[a.b])
                if do_norm:
                    y = xn[kc % 2]
                    kb.op("dve", lambda e: e.scalar_tensor_tensor(out=y.t[:], in0=a.t[:], scalar=gT.t[:, kc:kc + 1],
                                                                  in1=rstd.t[:], op0=ALU.mult, op1=ALU.mult),
                          reads=[a.b, gT.b, rstd.b], writes=[y.b])
                else:
                    y = a
                for t2 in range(2):
                    ps = C.ps[n % 8]
                    n += 1

                    def mm(e):
                        ins = None
                        for q in range(4):
                            tt = t2 * 4 + q
                            ins = e.transpose(ps.t[:, q * 128:(q + 1) * 128], y.t[:, tt * 128:(tt + 1) * 128], C.ident.t[:])
                        return ins
                    kb.op("pe", mm, reads=[y.b, C.ident.b], writes=[ps.b], self_sync=False)
                    o = s.t[:, t2 * 4:(t2 + 1) * 4, j * 128:(j + 1) * 128]
                    i_ = ps.t[:].rearrange("p (q d) -> p q d", q=4)
                    if (j + t2) % 2:
                        kb.op("act", lambda e: e.copy(o, i_), reads=[ps.b], writes=[s.b])
                    else:
                        kb.op("dve", lambda e: e.tensor_copy(o, i_), reads=[ps.b], writes=[s.b])
            kb.dma("sp", out_ap[:, k4 * 512:(k4 + 1) * 512].rearrange("(t p) d -> p t d", p=128), s.t[:],
                   reads=[s.b], writes=[C.outb])
        kb.barrier()


def rms_stats(kb, C, es, rstd):
    xc = [kb.sb(es, "rs_xc%d" % i, [128, TOK], F32) for i in range(3)]
    sq = [kb.sb(es, "rs_sq%d" % i, [128, TOK], F32) for i in range(2)]
    p0, p1 = C.ps[0], C.ps[1]
    for kc in range(NKC):
        a = xc[kc % 3]
        kb.dma("sp", a.t[:], C.xT[kc], reads=C.xTb[kc], writes=[a.b])
        s = sq[kc % 2]
        kb.op("act", lambda e: e.activation(out=s.t[:], in_=a.t[:], func=AF.Square), reads=[a.b], writes=[s.b])
        for th, ps in enumerate((p0, p1)):
            kb.op("pe", lambda e: e.matmul(ps.t[:], C.ones.t[:], s.t[:, th * 512:(th + 1) * 512],
                                           start=(kc == 0), stop=(kc == NKC - 1)),
                  reads=[s.b, C.ones.b], writes=[ps.b], self_sync=False)
    for th, ps in enumerate((p0, p1)):
        kb.op("act", lambda e: e.activation(out=rstd.t[:, th * 512:(th + 1) * 512], in_=ps.t[:], func=AF.Sqrt,
                                            bias=EPS, scale=1.0 / D),
              reads=[ps.b], writes=[rstd.b])
    kb.op("dve", lambda e: e.reciprocal(out=rstd.t[:], in_=rstd.t[:]), reads=[rstd.b], writes=[rstd.b])


def rmsnorm_to_hT(kb, C, gT):
    with ExitStack() as es:
        rstd = kb.sb(es, "rn_rstd", [128, TOK], F32)
        rms_stats(kb, C, es, rstd)
        xc = [kb.sb(es, "rn_xc%d" % i, [128, TOK], F32) for i in range(3)]
        for kc in range(NKC):
            a = xc[kc % 3]
            kb.dma("sp", a.t[:], C.xT[kc], reads=C.xTb[kc], writes=[a.b])
            eng = "dve"
            kb.op(eng, lambda e: e.scalar_tensor_tensor(out=C.hT.t[:, kc, :], in0=a.t[:], scalar=gT.t[:, kc:kc + 1],
                                                        in1=rstd.t[:], op0=ALU.mult, op1=ALU.mult),
                  reads=[a.b, gT.b, rstd.b], writes=[C.hTb[kc]])
        kb.barrier()


FFN_GROUPS = [15, 15, 14, 14, 14, 14]


def ffn_stage(kb, C, gT, w_in, w_out, bg_per_group=0):
    with ExitStack() as es:
        C.hT = kb.sb(es, "f_hT", [128, NKC, TOK], BF16)
        C.hTb = [Buf() for _ in range(NKC)]
        rmsnorm_to_hT(kb, C, gT)
        hT = C.hT
        GM = max(FFN_GROUPS)
        aT = kb.sb(es, "f_aT", [128, GM, TOK], BF16)
        aTb = [Buf() for _ in range(GM)]
        wi = [(kb.sb(es, "f_wg%d" % i, [128, NKC, 128], BF16), kb.sb(es, "f_wu%d" % i, [128, NKC, 128], BF16)) for i in range(2)]
        wo = [kb.sb(es, "f_wo%d" % i, [128, GM, 512], BF16) for i in range(2)]
        sg = [kb.sb(es, "f_sg%d" % i, [128, 512], BF16) for i in range(2)]
        yb = [kb.sb(es, "f_y%d" % i, [128, 512], F32) for i in range(4)]
        it = 0
        ust = 0
        dstp = 0
        f0 = 0
        hreads = list(C.hTb)
        for G in FFN_GROUPS:
            for j in range(G):
                fc = f0 + j
                wg, wu = wi[it % 2]
                it += 1
                for s_, wt_ in ((0, wg), (1, wu)):
                    r0 = (2 * fc + s_) * 128
                    kb.dma("sp", wt_.t[:], w_in.t[r0:r0 + 128, :].rearrange("p (kc f) -> p kc f", kc=NKC),
                           reads=w_in.need_rows(kb, r0, r0 + 128), writes=[wt_.b])
                kb.bg(bg_per_group)
                for th in range(2):
                    pg = C.ps[(ust % 2) * 2]
                    pu = C.ps[(ust % 2) * 2 + 1]
                    s = sg[ust % 2]
                    ust += 1
                    tsl = slice(th * 512, (th + 1) * 512)

                    def mmg(e, w=wg, p=pg):
                        ins = None
                        for kc in range(NKC):
                            ins = e.matmul(p.t[:], w.t[:, kc, :], hT.t[:, kc, tsl], start=(kc == 0), stop=(kc == NKC - 1))
                        return ins
                    kb.op("pe", mmg, reads=[wg.b] + hreads, writes=[pg.b], self_sync=False)
                    kb.op("act", lambda e: e.activation(out=s.t[:], in_=pg.t[:], func=AF.Silu), reads=[pg.b], writes=[s.b])
                    kb.op("pe", lambda e: mmg(e, wu, pu), reads=[wu.b] + hreads, writes=[pu.b], self_sync=False)
                    kb.op("dve", lambda e: e.tensor_tensor(out=aT.t[:, j, tsl], in0=s.t[:], in1=pu.t[:], op=ALU.mult),
                          reads=[s.b, pu.b], writes=[aTb[j]])
            for db in range(8):
                w = wo[dstp % 2]
                kb.dma("sp", w.t[:, :G, :],
                       w_out.t[f0 * 128:(f0 + G) * 128, db * 512:(db + 1) * 512].rearrange("(g p) d -> p g d", p=128),
                       reads=w_out.need_rows(kb, f0 * 128, (f0 + G) * 128), writes=[w.b])
                for dcl in range(4):
                    dc = db * 4 + dcl
                    for th in range(2):
                        ps = C.ps[4 + dstp % 4]
                        y = yb[dstp % 4]
                        dstp += 1
                        tsl = slice(th * 512, (th + 1) * 512)

                        def mmd(e):
                            ins = None
                            for j in range(G):
                                ins = e.matmul(ps.t[:], w.t[:, j, dcl * 128:(dcl + 1) * 128], aT.t[:, j, tsl],
                                               start=(j == 0), stop=(j == G - 1))
                            return ins
                        kb.op("pe", mmd, reads=[w.b] + aTb[:G], writes=[ps.b], self_sync=False)
                        if dstp % 2:
                            kb.op("act", lambda e: e.activation(out=y.t[:], in_=ps.t[:], func=AF.Copy, scale=0.5),
                                  reads=[ps.b], writes=[y.b])
                        else:
                            kb.op("dve", lambda e: e.tensor_scalar(out=y.t[:], in0=ps.t[:], scalar1=0.5, scalar2=None, op0=ALU.mult),
                                  reads=[ps.b], writes=[y.b])
                        kb.dma("pool", C.xT[dc][:, tsl], y.t[:], accum_op=ALU.add, reads=[y.b], writes=[C.xTb[dc][th]])
                kb.bg(bg_per_group)
            f0 += G
        kb.barrier()


def rms_hT_n(kb, C, src_fn, N, gT, hT, hTb, col0, tagp):
    with ExitStack() as es:
        rstd = kb.sb(es, tagp + "_rstd", [128, N], F32)
        xc = [kb.sb(es, tagp + "_xc%d" % i, [128, N], F32) for i in range(3)]
        sq = [kb.sb(es, tagp + "_sq%d" % i, [128, N], F32) for i in range(2)]
        ps = C.ps[0]

        def get(kc):
            ap, bufs, is_dram = src_fn(kc)
            if is_dram:
                a = xc[kc % 3]
                kb.dma("sp", a.t[:], ap, reads=bufs, writes=[a.b])
                return a.t[:], [a.b]
            return ap, bufs
        for kc in range(NKC):
            xin, xb = get(kc)
            s = sq[kc % 2]
            kb.op("act", lambda e: e.activation(out=s.t[:], in_=xin, func=AF.Square), reads=xb, writes=[s.b])
            kb.op("pe", lambda e: e.matmul(ps.t[:, :N], C.ones.t[:], s.t[:], start=(kc == 0), stop=(kc == NKC - 1)),
                  reads=[s.b, C.ones.b], writes=[ps.b], self_sync=False)
        kb.op("act", lambda e: e.activation(out=rstd.t[:], in_=ps.t[:, :N], func=AF.Sqrt, bias=EPS, scale=1.0 / D),
              reads=[ps.b], writes=[rstd.b])
        kb.op("dve", lambda e: e.reciprocal(out=rstd.t[:], in_=rstd.t[:]), reads=[rstd.b], writes=[rstd.b])
        for kc in range(NKC):
            xin, xb = get(kc)
            kb.op("dve", lambda e: e.scalar_tensor_tensor(out=hT.t[:, kc, col0:col0 + N], in0=xin, scalar=gT.t[:, kc:kc + 1],
                                                          in1=rstd.t[:], op0=ALU.mult, op1=ALU.mult),
                  reads=xb + [gT.b, rstd.b], writes=[hTb[kc]])
        kb.barrier()


def load_wtile(kb, w, col, width, wt):
    kb.bg(1)
    kb.dma("sp", wt.t[:, :, :width], w.t[:, col:col + width].rearrange("(kc p) f -> p kc f", p=128),
           reads=[w.b], writes=[wt.b])


def mm_proj(kb, ps, width, N, wt, hT, hTb, c0):
    def mm(e):
        ins = None
        for kc in range(NKC):
            ins = e.matmul(ps.t[:width, :N], wt.t[:, kc, :width], hT.t[:, kc, c0:c0 + N], start=(kc == 0), stop=(kc == NKC - 1))
        return ins
    kb.op("pe", mm, reads=[wt.b] + list(hTb), writes=[ps.b], self_sync=False)


def ln_fm(kb, C, y, ybufs, NCH, N, gT, bT, out_fn, func, tagp):
    NC = NCH * 128
    with ExitStack() as es:
        sq = [kb.sb(es, tagp + "_sq%d" % i, [128, N], F32) for i in range(2)]
        tm = [kb.sb(es, tagp + "_tm%d" % i, [128, N], F32) for i in range(2)]
        mean = kb.sb(es, tagp + "_mean", [128, N], F32)
        rstd = kb.sb(es, tagp + "_rstd", [128, N], F32)
        msq = kb.sb(es, tagp + "_msq", [128, N], F32)
        p1, p2 = C.ps[0], C.ps[1]
        for fc in range(NCH):
            kb.op("pe", lambda e: e.matmul(p1.t[:, :N], C.ones.t[:], y.t[:, fc, :], start=(fc == 0), stop=(fc == NCH - 1)),
                  reads=[ybufs[fc], C.ones.b], writes=[p1.b], self_sync=False)
            s = sq[fc % 2]
            kb.op("act", lambda e: e.activation(out=s.t[:], in_=y.t[:, fc, :], func=AF.Square), reads=[ybufs[fc]], writes=[s.b])
            kb.op("pe", lambda e: e.matmul(p2.t[:, :N], C.ones.t[:], s.t[:], start=(fc == 0), stop=(fc == NCH - 1)),
                  reads=[s.b, C.ones.b], writes=[p2.b], self_sync=False)
        kb.op("act", lambda e: e.activation(out=mean.t[:], in_=p1.t[:, :N], func=AF.Copy, scale=1.0 / NC), reads=[p1.b], writes=[mean.b])
        kb.op("dve", lambda e: e.tensor_tensor(out=msq.t[:], in0=mean.t[:], in1=mean.t[:], op=ALU.mult), reads=[mean.b], writes=[msq.b])
        kb.op("dve", lambda e: e.scalar_tensor_tensor(out=rstd.t[:], in0=p2.t[:, :N], scalar=1.0 / NC, in1=msq.t[:],
                                                      op0=ALU.mult, op1=ALU.subtract), reads=[p2.b, msq.b], writes=[rstd.b])
        kb.op("act", lambda e: e.activation(out=rstd.t[:], in_=rstd.t[:], func=AF.Sqrt, bias=EPS, scale=1.0), reads=[rstd.b], writes=[rstd.b])
        kb.op("dve", lambda e: e.reciprocal(out=rstd.t[:], in_=rstd.t[:]), reads=[rstd.b], writes=[rstd.b])
        for fc in range(NCH):
            t = tm[fc % 2]
            kb.op("dve", lambda e: e.tensor_tensor(out=t.t[:], in0=y.t[:, fc, :], in1=mean.t[:], op=ALU.subtract),
                  reads=[ybufs[fc], mean.b], writes=[t.b])
            kb.op("pool", lambda e: e.tensor_tensor(out=t.t[:], in0=t.t[:], in1=rstd.t[:], op=ALU.mult),
                  reads=[t.b, rstd.b], writes=[t.b])
            o_ap, o_buf = out_fn(fc)
            kb.op("act", lambda e: e.activation(out=o_ap, in_=t.t[:], func=func, scale=gT.t[:, fc:fc + 1], bias=bT.t[:, fc:fc + 1]),
                  reads=[t.b, gT.b, bT.b], writes=[o_buf])
        kb.barrier()


def out_proj(kb, C, w, rhs_fn, n0, N, tagp, nk=NKC, krow0=0):
    with ExitStack() as es:
        wo = [kb.sb(es, tagp + "_wo%d" % i, [128, nk, 256], BF16) for i in range(2)]
        yb = [kb.sb(es, tagp + "_y%d" % i, [128, N], F32) for i in range(4)]
        rh = [rhs_fn(fc) for fc in range(nk)]
        cnt = 0
        for blk in range(16):
            kb.bg(1)
            wt = wo[blk % 2]
            kb.dma("sp", wt.t[:], w.t[krow0 * 128:(krow0 + nk) * 128, blk * 256:(blk + 1) * 256].rearrange("(g p) d -> p g d", p=128),
                   reads=[w.b], writes=[wt.b])
            for dcl in range(2):
                dc = blk * 2 + dcl
                ps = C.ps[4 + cnt % 4]
                y = yb[cnt % 4]
                cnt += 1

                def mm(e):
                    ins = None
                    for fc in range(nk):
                        ins = e.matmul(ps.t[:, :N], wt.t[:, fc, dcl * 128:(dcl + 1) * 128], rh[fc][0], start=(fc == 0), stop=(fc == nk - 1))
                    return ins
                kb.op("pe", mm, reads=[wt.b] + [r[1] for r in rh], writes=[ps.b], self_sync=False)
                if cnt % 2:
                    kb.op("act", lambda e: e.copy(y.t[:], ps.t[:, :N]), reads=[ps.b], writes=[y.b])
                else:
                    kb.op("dve", lambda e: e.tensor_copy(y.t[:], ps.t[:, :N]), reads=[ps.b], writes=[y.b])
                kb.dma("pool", C.xT[dc][:, n0:n0 + N], y.t[:], accum_op=ALU.add, reads=[y.b], writes=[C.xTb[dc][n0 // 512]])
        kb.barrier()


def gather_xtail(kb, C, nc, tag, ntail=32):
    src = nc.dram_tensor(tag + "_xtail_s", [D, ntail], F32).ap()
    allt = nc.dram_tensor(tag + "_xtail_a", [4 * D, ntail], F32).ap()
    sb_, ab_ = Buf(), Buf()
    for k8 in range(4):
        kb.dma("sp", src[k8 * 1024:(k8 + 1) * 1024, :].rearrange("(kc p) t -> kc p t", p=128),
               C.xT[k8 * 8:(k8 + 1) * 8, :, TOK - ntail:TOK],
               reads=[b for kc in range(k8 * 8, (k8 + 1) * 8) for b in C.xTb[kc]], writes=[sb_])
    kb.cc("AllGather", G4, src, allt, reads=[sb_], writes=[ab_])
    return allt, ab_


def halo_x(kb, C, es, allt, ab_, ntail, tagp):
    xt = kb.sb(es, tagp + "_xt4", [128, 4, NKC, ntail], F32)
    xh = kb.sb(es, tagp + "_xh", [128, NKC, ntail], F32)
    for q in range(4):
        kb.dma("sp", xt.t[:, q], allt[q * D:(q + 1) * D, :].rearrange("(kc p) t -> p kc t", p=128), reads=[ab_], writes=[xt.b])
    kb.op("dve", lambda e: e.tensor_scalar(out=xh.t[:], in0=xt.t[:, 0], scalar1=C.sel.t[:, 0:1], scalar2=None, op0=ALU.mult),
          reads=[xt.b, C.sel.b], writes=[xh.b])
    for q in range(1, 4):
        kb.op("dve", lambda e: e.scalar_tensor_tensor(out=xh.t[:], in0=xt.t[:, q], scalar=C.sel.t[:, q:q + 1], in1=xh.t[:],
                                                      op0=ALU.mult, op1=ALU.add),
              reads=[xt.b, C.sel.b, xh.b], writes=[xh.b])
    return xh


HB = 32
NH = 512
KW = 31


def mixer_ab(kb, C, nc, P, w_in, w_out):
    WA = 2048
    allt, ab_ = gather_xtail(kb, C, nc, "ab")
    with ExitStack() as es:
        hB = kb.sb(es, "ab_hB", [128, 16, HB + NH], BF16)
        hBb = [Buf() for _ in range(16)]
        yA = kb.sb(es, "ab_yA", [128, 16, NH], BF16)
        yAb = [Buf() for _ in range(16)]
        yB = kb.sb(es, "ab_yB", [128, 16, NH], BF16)
        yBb = [Buf() for _ in range(16)]
        zu = kb.sb(es, "ab_zu", [128, 16, NH], BF16)
        zub = [Buf() for _ in range(16)]
        zv = kb.sb(es, "ab_zv", [128, 16, NH], F32)
        zvb = [Buf() for _ in range(16)]
        WmT = kb.sb(es, "ab_WmT", [128, 8, 128], BF16)
        for g in range(8):
            kb.op("dve", lambda e: e.tensor_tensor(out=WmT.t[:, g, :], in0=P["wspT"].t[:, g, :], in1=P["tri"].t[:], op=ALU.mult),
                  reads=[P["wspT"].b, P["tri"].b], writes=[WmT.b])
        bank = [0]

        def nextps():
            p = C.ps[2 + bank[0] % 6]
            bank[0] += 1
            return p

        def glu_proj(hT, hTb, N, dst0, wr, sgr):
            for fc in range(16):
                wa = wr[(2 * fc) % 3]
                wg = wr[(2 * fc + 1) % 3]
                load_wtile(kb, w_in, 2 * WA + fc * 128, 128, wa)
                pa = nextps()
                mm_proj(kb, pa, 128, N, wa, hT, hTb, 0)
                load_wtile(kb, w_in, 3 * WA + fc * 128, 128, wg)
                pg = nextps()
                mm_proj(kb, pg, 128, N, wg, hT, hTb, 0)
                sg = sgr[fc % 2]
                kb.op("act", lambda e: e.activation(out=sg.t[:, :N], in_=pg.t[:, :N], func=AF.Sigmoid), reads=[pg.b], writes=[sg.b])
                kb.op("dve", lambda e: e.tensor_tensor(out=hB.t[:, fc, dst0:dst0 + N], in0=sg.t[:, :N], in1=pa.t[:, :N], op=ALU.mult),
                      reads=[sg.b, pa.b], writes=[hBb[fc]])

        with ExitStack() as es1:
            xh = halo_x(kb, C, es1, allt, ab_, HB, "abh")
            hTh = kb.sb(es1, "abh_hT", [128, NKC, HB], BF16)
            hThb = [Buf() for _ in range(NKC)]
            rms_hT_n(kb, C, lambda kc: (xh.t[:, kc, :], [xh.b], False), HB, P["gmix0"], hTh, hThb, 0, "abh_rn")
            wr = [kb.sb(es1, "abh_w%d" % i, [128, NKC, 128], BF16) for i in range(3)]
            sgr = [kb.sb(es1, "abh_sg%d" % i, [128, NH], F32) for i in range(2)]
            glu_proj(hTh, hThb, HB, 0, wr, sgr)
            kb.barrier()

        for half in range(2):
            n0 = half * NH
            with ExitStack() as es1:
                hT = kb.sb(es1, "ab_hT", [128, NKC, NH], BF16)
                hTb = [Buf() for _ in range(NKC)]
                rms_hT_n(kb, C, lambda kc: (C.xT[kc][:, n0:n0 + NH], C.xTb[kc], True), NH, P["gmix0"], hT, hTb, 0, "ab_rn")
                wr = [kb.sb(es1, "ab_w%d" % i, [128, NKC, 128], BF16) for i in range(3)]
                sgr = [kb.sb(es1, "ab_sg%d" % i, [128, NH], F32) for i in range(2)]
                glu_proj(hT, hTb, NH, HB, wr, sgr)
                for fc in range(16):
                    wu = wr[(2 * fc) % 3]
                    wv = wr[(2 * fc + 1) % 3]
                    load_wtile(kb, w_in, fc * 128, 128, wu)
                    pu = nextps()
                    mm_proj(kb, pu, 128, NH, wu, hT, hTb, 0)
                    kb.op("act", lambda e: e.activation(out=zu.t[:, fc, :], in_=pu.t[:, :NH], func=AF.Gelu), reads=[pu.b], writes=[zub[fc]])
                    load_wtile(kb, w_in, WA + fc * 128, 128, wv)
                    pv = nextps()
                    mm_proj(kb, pv, 128, NH, wv, hT, hTb, 0)
                    kb.op("act", lambda e: e.activation(out=zv.t[:, fc, :], in_=pv.t[:, :NH], func=AF.Gelu), reads=[pv.b], writes=[zvb[fc]])
                kb.barrier()
            ln_fm(kb, C, zv, zvb, 16, NH, P["lnag"], P["lnab"], lambda fc: (zv.t[:, fc, :], zvb[fc]), AF.Identity, "ab_lna")
            with ExitStack() as es1:
                vt = [kb.sb(es1, "ab_vt%d" % i, [128, 128], BF16) for i in range(3)]
                tt_ = [kb.sb(es1, "ab_t%d" % i, [128, 128], F32) for i in range(3)]
                n = 0
                for tq in range(NH // 128):
                    tsl = slice(tq * 128, (tq + 1) * 128)
                    for fc in range(16):
                        g = fc // 2
                        pt = nextps()
                        kb.op("pe", lambda e: e.transpose(pt.t[:, :128], zv.t[:, fc, tsl], C.ident.t[:]),
                              reads=[zvb[fc], C.ident.b], writes=[pt.b], self_sync=False)
                        v_ = vt[n % 3]
                        t_ = tt_[n % 3]
                        n += 1
                        kb.op("act", lambda e: e.copy(v_.t[:], pt.t[:, :128]), reads=[pt.b], writes=[v_.b])
                        ps = nextps()
                        kb.op("pe", lambda e: e.matmul(ps.t[:, :128], v_.t[:], WmT.t[:, g, :], start=True, stop=True),
                              reads=[v_.b, WmT.b], writes=[ps.b], self_sync=False)
                        kb.op("dve", lambda e: e.tensor_tensor(out=t_.t[:], in0=ps.t[:, :128], in1=P["bspbc"].t[:, g, :], op=ALU.add),
                              reads=[ps.b, P["bspbc"].b], writes=[t_.b])
                        kb.op("pool", lambda e: e.tensor_tensor(out=yA.t[:, fc, tsl], in0=t_.t[:], in1=zu.t[:, fc, tsl], op=ALU.mult),
                              reads=[t_.b, zub[fc]], writes=[yAb[fc]])
                kb.barrier()
            with ExitStack() as es1:
                for fc in range(16):
                    cw = P["convw"].t
                    kb.op("dve", lambda e: e.tensor_scalar(out=zv.t[:, fc, :], in0=hB.t[:, fc, 2:2 + NH], scalar1=cw[:, fc, 0:1],
                                                           scalar2=P["convb"].t[:, fc:fc + 1], op0=ALU.mult, op1=ALU.add),
                          reads=[hBb[fc], P["convw"].b, P["convb"].b], writes=[zvb[fc]])
                    for k in range(1, KW):
                        kb.op("dve", lambda e: e.scalar_tensor_tensor(out=zv.t[:, fc, :], in0=hB.t[:, fc, 2 + k:2 + k + NH], scalar=cw[:, fc, k:k + 1],
                                                                      in1=zv.t[:, fc, :], op0=ALU.mult, op1=ALU.add),
                              reads=[hBb[fc], zvb[fc]], writes=[zvb[fc]])
                    kb.op("pool", lambda e: e.tensor_copy(hB.t[:, fc, 0:HB], hB.t[:, fc, NH:NH + HB]), reads=[hBb[fc]], writes=[hBb[fc]])
            ln_fm(kb, C, zv, zvb, 16, NH, P["lnbg"], P["lnbb"], lambda fc: (yB.t[:, fc, :], yBb[fc]), AF.Silu, "ab_lnb")
            out_proj(kb, C, w_out, lambda fc: ((yA.t[:, fc, :], yAb[fc]) if fc < 16 else (yB.t[:, fc - 16, :], yBb[fc - 16])), n0, NH, "ab_o")
        kb.barrier()


NEG = -1.0e30
RNK = 1152
RNPAD = 512


def rms_fm(kb, C, z, zb, NCH, N, gT, out_fn, tagp):
    NC = NCH * 128
    with ExitStack() as es:
        sq = [kb.sb(es, tagp + "_sq%d" % i, [128, N], F32) for i in range(2)]
        rstd = kb.sb(es, tagp + "_rstd", [128, N], F32)
        p1 = C.ps[0]
        for fc in range(NCH):
            s = sq[fc % 2]
            kb.op("act", lambda e: e.activation(out=s.t[:], in_=z.t[:, fc, :], func=AF.Square), reads=[zb[fc]], writes=[s.b])
            kb.op("pe", lambda e: e.matmul(p1.t[:, :N], C.ones.t[:], s.t[:], start=(fc == 0), stop=(fc == NCH - 1)),
                  reads=[s.b, C.ones.b], writes=[p1.b], self_sync=False)
        kb.op("act", lambda e: e.activation(out=rstd.t[:], in_=p1.t[:, :N], func=AF.Sqrt, bias=EPS, scale=1.0 / NC), reads=[p1.b], writes=[rstd.b])
        kb.op("dve", lambda e: e.reciprocal(out=rstd.t[:], in_=rstd.t[:]), reads=[rstd.b], writes=[rstd.b])
        for fc in range(NCH):
            o_ap, o_buf = out_fn(fc)
            kb.op("dve", lambda e: e.scalar_tensor_tensor(out=o_ap, in0=z.t[:, fc, :], scalar=gT.t[:, fc:fc + 1], in1=rstd.t[:],
                                                          op0=ALU.mult, op1=ALU.mult), reads=[zb[fc], gT.b, rstd.b], writes=[o_buf])
        kb.barrier()


def mixer_cd(kb, C, nc, P, W):
    w_in, w_out, w_pool, w_uq, w_qidx, w_uk, w_uv = W
    SC = 128.0 ** -0.5
    WI = 32.0 ** -0.5 * 64.0 ** -0.5
    allt, ab_ = gather_xtail(kb, C, nc, "cd")
    bf = BF16
    ckvT_s = T(nc.dram_tensor("cd_ckvT_s", [512, TOK], bf).ap())
    ckvT_a = T(nc.dram_tensor("cd_ckvT_a", [4 * 512, TOK], bf).ap())
    ckvm_s = T(nc.dram_tensor("cd_ckvm_s", [TOK, 512], bf).ap())
    ckvm_a = T(nc.dram_tensor("cd_ckvm_a", [4 * TOK, 512], bf).ap())
    kidx_s = T(nc.dram_tensor("cd_kidx_s", [64, TOK], bf).ap())
    kidx_a = T(nc.dram_tensor("cd_kidx_a", [4 * 64, TOK], bf).ap())
    q_d = T(nc.dram_tensor("cd_q_d", [16, 128, TOK], bf).ap())
    qi_d = T(nc.dram_tensor("cd_qi_d", [16, 128, TOK], bf).ap())
    RN_t = nc.dram_tensor("cd_RN_d", [16, RNK], bf)
    RN_d = T(RN_t.ap())
    bank = [0]

    def nextps():
        p = C.ps[2 + bank[0] % 6]
        bank[0] += 1
        return p

    with ExitStack() as es:
        ckvT_own = kb.sb(es, "cd_ckvTo", [128, 4, TOK], bf)
        ckvTob = [Buf() for _ in range(4)]
        kidx_own = kb.sb(es, "cd_kidxo", [64, TOK], bf)
        w_tm = kb.sb(es, "cd_wtm", [128, 8, 32], F32)
        identb = kb.sb(es, "cd_identb", [128, 128], bf)
        onesb = kb.sb(es, "cd_onesb", [128, 128], bf)
        Jb = kb.sb(es, "cd_Jb", [128, 128], bf)
        kb.op("act", lambda e: e.copy(identb.t[:], C.ident.t[:]), reads=[C.ident.b], writes=[identb.b])
        kb.op("act", lambda e: e.copy(Jb.t[:], P["Jf"].t[:]), reads=[P["Jf"].b], writes=[Jb.b])
        kb.op("dve", lambda e: e.memset(onesb.t[:], 1.0), writes=[onesb.b])
        with ExitStack() as es1:
            rn = kb.sb(es1, "cd_rn", [16, RNK], bf)
            for ci, (c0, n) in enumerate(((0, 512), (512, 512), (1024, 128))):
                ps = nextps()
                kb.op("pe", lambda e: e.matmul(ps.t[:16, :n], P["rb"].t[0:32, :], P["OHk"].t[0:32, c0:c0 + n], start=True, stop=True),
                      reads=[P["rb"].b, P["OHk"].b], writes=[ps.b], self_sync=False)
                kb.op("act", lambda e: e.activation(out=rn.t[:, c0:c0 + n], in_=ps.t[:16, :n], func=AF.Copy, scale=1.0 / SC),
                      reads=[ps.b], writes=[rn.b])
            kb.dma("sp", RN_d.t[:, :], rn.t[:], reads=[rn.b], writes=[RN_d.b])
            kb.barrier()

        with ExitStack() as esP:
            pT = kb.sb(esP, "cd_pT", [128, 16, HB + NH], F32)
            pTb = [Buf() for _ in range(16)]
            wpool = kb.sb(esP, "cd_wpool", [128, 16, 512], bf)
            kb.dma("sp", wpool.t[:], w_pool.t[:, :].rearrange("(g p) e -> p g e", p=128), reads=[w_pool.b], writes=[wpool.b])

            def p_proj(hT, hTb, N, dst0, wr):
                for fc in range(16):
                    wt = wr[fc % 3]
                    load_wtile(kb, w_in, fc * 128, 128, wt)
                    ps = nextps()
                    mm_proj(kb, ps, 128, N, wt, hT, hTb, 0)
                    if fc % 2:
                        kb.op("act", lambda e: e.copy(pT.t[:, fc, dst0:dst0 + N], ps.t[:, :N]), reads=[ps.b], writes=[pTb[fc]])
                    else:
                        kb.op("dve", lambda e: e.tensor_copy(pT.t[:, fc, dst0:dst0 + N], ps.t[:, :N]), reads=[ps.b], writes=[pTb[fc]])

            with ExitStack() as es1:
                xh = halo_x(kb, C, es1, allt, ab_, HB, "cdh")
                hTh = kb.sb(es1, "cdh_hT", [128, NKC, HB], bf)
                hThb = [Buf() for _ in range(NKC)]
                rms_hT_n(kb, C, lambda kc: (xh.t[:, kc, :], [xh.b], False), HB, P["gmix1"], hTh, hThb, 0, "cdh_rn")
                wr = [kb.sb(es1, "cdh_w%d" % i, [128, NKC, 128], bf) for i in range(3)]
                p_proj(hTh, hThb, HB, 0, wr)
                kb.barrier()

            for half in range(2):
                n0 = half * NH
                with ExitStack() as es1:
                    hT = kb.sb(es1, "cd_hT", [128, NKC, NH], bf)
                    hTb = [Buf() for _ in range(NKC)]
                    rms_hT_n(kb, C, lambda kc: (C.xT[kc][:, n0:n0 + NH], C.xTb[kc], True), NH, P["gmix1"], hT, hTb, 0, "cd_rn")
                    wr = [kb.sb(es1, "cd_w%d" % i, [128, NKC, 128], bf) for i in range(3)]
                    zq = kb.sb(es1, "cd_zq", [128, 8, NH], F32)
                    zqb = [Buf() for _ in range(8)]
                    zkv = kb.sb(es1, "cd_zkv", [128, 4, NH], F32)
                    zkvb = [Buf() for _ in range(4)]
                    cqT = kb.sb(es1, "cd_cqT", [128, 8, NH], bf)
                    cqb = [Buf() for _ in range(8)]
                    p_proj(hT, hTb, NH, HB, wr)
                    for fc in range(12):
                        wt = wr[fc % 3]
                        load_wtile(kb, w_in, 2048 + fc * 128, 128, wt)
                        ps = nextps()
                        mm_proj(kb, ps, 128, NH, wt, hT, hTb, 0)
                        dst, db_ = (zq.t[:, fc, :], zqb[fc]) if fc < 8 else (zkv.t[:, fc - 8, :], zkvb[fc - 8])
                        kb.op("act", lambda e: e.copy(dst, ps.t[:, :NH]), reads=[ps.b], writes=[db_])
                    wt = wr[0]
                    load_wtile(kb, w_in, 3584, 96, wt)
                    ps = nextps()
                    mm_proj(kb, ps, 64, NH, wt, hT, hTb, 0)
                    kb.op("act", lambda e: e.copy(kidx_own.t[:, n0:n0 + NH], ps.t[:64, :NH]), reads=[ps.b], writes=[kidx_own.b])
                    for tq in range(4):
                        ps = nextps()

                        def mmw(e):
                            ins = None
                            for kc in range(NKC):
                                ins = e.matmul(ps.t[:, :32], hT.t[:, kc, tq * 128:(tq + 1) * 128], wt.t[:, kc, 64:96],
                                               start=(kc == 0), stop=(kc == NKC - 1))
                            return ins
                        kb.op("pe", mmw, reads=[wt.b] + hTb, writes=[ps.b], self_sync=False)
                        kb.op("act", lambda e: e.activation(out=w_tm.t[:, half * 4 + tq, :], in_=ps.t[:, :32], func=AF.Copy, scale=WI),
                              reads=[ps.b], writes=[w_tm.b])
                    kb.barrier()
                    rms_fm(kb, C, zq, zqb, 8, NH, P["gcq"], lambda fc: (cqT.t[:, fc, :], cqb[fc]), "cd_nq")
                    rms_fm(kb, C, zkv, zkvb, 4, NH, P["gckv"], lambda fc: (ckvT_own.t[:, fc, n0:n0 + NH], ckvTob[fc]), "cd_nkv")
                    w8 = [kb.sb(es1, "cd_w8%d" % i, [128, 8, 128], bf) for i in range(2)]
                    stg = [kb.sb(es1, "cd_stg%d" % i, [128, NH], bf) for i in range(3)]
                    for i in range(32):
                        wsrc, dst_d, hh = (w_uq, q_d, i) if i < 16 else (w_qidx, qi_d, i - 16)
                        wt8 = w8[i % 2]
                        kb.dma("sp", wt8.t[:], wsrc.t[:, hh * 128:(hh + 1) * 128].rearrange("(kc p) f -> p kc f", p=128),
                               reads=[wsrc.b], writes=[wt8.b])
                        ps = nextps()

                        def mm8(e):
                            ins = None
                            for kc in range(8):
                                ins = e.matmul(ps.t[:, :NH], wt8.t[:, kc, :], cqT.t[:, kc, :], start=(kc == 0), stop=(kc == 7))
                            return ins
                        kb.op("pe", mm8, reads=[wt8.b] + cqb, writes=[ps.b], self_sync=False)
                        sg_ = stg[i % 3]
                        if i % 2:
                            kb.op("act", lambda e: e.copy(sg_.t[:], ps.t[:, :NH]), reads=[ps.b], writes=[sg_.b])
                        else:
                            kb.op("dve", lambda e: e.tensor_copy(sg_.t[:], ps.t[:, :NH]), reads=[ps.b], writes=[sg_.b])
                        kb.dma("sp", dst_d.t[hh][:, n0:n0 + NH], sg_.t[:], reads=[sg_.b], writes=[dst_d.b])
                    kb.barrier()
                with ExitStack() as es1:
                    dT = kb.sb(es1, "cd_dT", [128, 16, NH], bf)
                    dTb = [Buf() for _ in range(16)]
                    yC = kb.sb(es1, "cd_yC", [128, 16, NH], bf)
                    yCb = [Buf() for _ in range(16)]
                    invc = kb.sb(es1, "cd_invc", [128, 4, NH], F32)
                    kb.dma("sp", invc.t[:], P["invcnt_d"].t[:, :].rearrange("p (g t) -> p g t", g=4)[:, :, n0:n0 + NH], writes=[invc.b])
                    ta = [kb.sb(es1, "cd_ta%d" % i, [128, HB + NH], F32) for i in range(2)]
                    tb = [kb.sb(es1, "cd_tb%d" % i, [128, HB + NH], F32) for i in range(2)]
                    LT = HB + NH
                    for fc in range(16):
                        gi = fc // 4
                        eng = "dve" if fc % 2 else "pool"
                        a_, b_ = ta[fc % 2], tb[fc % 2]
                        src = pT.t[:, fc, :]
                        srcb = pTb[fc]
                        cur, curb = src, srcb
                        sh = 1
                        bufs2 = [a_, b_]
                        for lv in range(gi + 1):
                            dstt = bufs2[lv % 2]
                            c_ap = cur
                            kb.op(eng, lambda e: e.tensor_tensor(out=dstt.t[:, 16:LT], in0=c_ap[:, 16:LT], in1=c_ap[:, 16 - sh:LT - sh], op=ALU.add),
                                  reads=[curb], writes=[dstt.b])
                            cur, curb = dstt.t[:], dstt.b
                            sh *= 2
                        m_ = bufs2[(gi + 1) % 2]
                        c_ap = cur
                        kb.op(eng, lambda e: e.tensor_tensor(out=m_.t[:, HB:LT], in0=c_ap[:, HB:LT], in1=invc.t[:, gi, :], op=ALU.mult),
                              reads=[curb, invc.b], writes=[m_.b])
                        kb.op(eng, lambda e: e.tensor_tensor(out=dT.t[:, fc, :], in0=m_.t[:, HB:LT], in1=src[:, HB:LT], op=ALU.subtract),
                              reads=[m_.b, srcb], writes=[dTb[fc]])
                        kb.op(eng, lambda e: e.tensor_copy(pT.t[:, fc, 0:HB], pT.t[:, fc, NH:NH + HB]), reads=[pTb[fc]], writes=[pTb[fc]])
                    for ec in range(16):
                        gi, el = ec // 4, ec % 4
                        ps = nextps()

                        def mmp(e):
                            ins = None
                            for cc in range(4):
                                ins = e.matmul(ps.t[:, :NH], wpool.t[:, gi * 4 + cc, el * 128:(el + 1) * 128], dT.t[:, gi * 4 + cc, :],
                                               start=(cc == 0), stop=(cc == 3))
                            return ins
                        kb.op("pe", mmp, reads=[wpool.b] + dTb[gi * 4:gi * 4 + 4], writes=[ps.b], self_sync=False)
                        kb.op("act", lambda e: e.activation(out=yC.t[:, ec, :], in_=ps.t[:, :NH], func=AF.Copy, scale=P["poolsc"].t[:, ec:ec + 1]),
                              reads=[ps.b, P["poolsc"].b], writes=[yCb[ec]])
                    out_proj(kb, C, w_out, lambda fc: (yC.t[:, fc, :], yCb[fc]), n0, NH, "cd_oc", nk=16, krow0=0)
            kb.barrier()

        with ExitStack() as es1:
            kb.dma("sp", ckvT_s.t[:, :].rearrange("(rc p) t -> p rc t", p=128), ckvT_own.t[:], reads=ckvTob, writes=[ckvT_s.b])
            kb.dma("sp", kidx_s.t[:, :], kidx_own.t[:], reads=[kidx_own.b], writes=[kidx_s.b])
            stg = [kb.sb(es1, "cd_tst%d" % i, [128, 512], bf) for i in range(2)]
            for tq in range(8):
                sg_ = stg[tq % 2]
                for rc in range(4):
                    ps = nextps()
                    kb.op("pe", lambda e: e.matmul(ps.t[:, :128], ckvT_own.t[:, rc, tq * 128:(tq + 1) * 128], identb.t[:], start=True, stop=True),
                          reads=[ckvTob[rc], identb.b], writes=[ps.b], self_sync=False)
                    kb.op("act" if rc % 2 else "dve",
                          (lambda e: e.copy(sg_.t[:, rc * 128:(rc + 1) * 128], ps.t[:, :128])) if rc % 2 else
                          (lambda e: e.tensor_copy(sg_.t[:, rc * 128:(rc + 1) * 128], ps.t[:, :128])),
                          reads=[ps.b], writes=[sg_.b])
                kb.dma("sp", ckvm_s.t[tq * 128:(tq + 1) * 128, :], sg_.t[:], reads=[sg_.b], writes=[ckvm_s.b])
            kb.cc("AllGather", G4, ckvT_s.t, ckvT_a.t, reads=[ckvT_s.b], writes=[ckvT_a.b])
            kb.cc("AllGather", G4, ckvm_s.t, ckvm_a.t, reads=[ckvm_s.b], writes=[ckvm_a.b])
            kb.cc("AllGather", G4, kidx_s.t, kidx_a.t, reads=[kidx_s.b], writes=[kidx_a.b])
            kb.barrier()

        def sel_combine(dst_fn, src_fn, ops_shape_ok=True):
            for sl in range(3):
                r = 3 - sl
                d_ap, d_b = dst_fn(sl)
                for q in range(4):
                    s_ap, s_b = src_fn(q)
                    sc = P["selr"].t[:, (r - 1) * 4 + q:(r - 1) * 4 + q + 1]
                    if q == 0:
                        kb.op("dve", lambda e: e.tensor_scalar(out=d_ap, in0=s_ap, scalar1=sc, scalar2=None, op0=ALU.mult),
                              reads=[s_b, P["selr"].b], writes=[d_b])
                    else:
                        kb.op("dve", lambda e: e.scalar_tensor_tensor(out=d_ap, in0=s_ap, scalar=sc, in1=d_ap, op0=ALU.mult, op1=ALU.add),
                              reads=[s_b, P["selr"].b, d_b], writes=[d_b])

        with ExitStack() as esK:
            maskT = kb.sb(esK, "cd_maskT", [128, 32, NH], bf)
            for hf in range(2):
                n0 = hf * NH
                b0 = 24 + 4 * hf
                JS = b0 + 4
                with ExitStack() as es1:
                    kidx_k = kb.sb(es1, "cd_kidxk", [128, 4, TOK], bf)
                    with ExitStack() as es2:
                        tmp = kb.sb(es2, "cd_kitmp", [128, 4, TOK], bf)
                        for hp in range(2):
                            kb.dma("sp", tmp.t[hp * 64:(hp + 1) * 64], kidx_a.t[:, :].rearrange("(q d) t -> d q t", q=4),
                                   reads=[kidx_a.b], writes=[tmp.b])
                            kb.dma("sp", kidx_k.t[hp * 64:(hp + 1) * 64, 3, :], kidx_s.t[:, :], reads=[kidx_s.b], writes=[kidx_k.b])
                        sel_combine(lambda sl: (kidx_k.t[:, sl, :], kidx_k.b), lambda q: (tmp.t[:, q, :], tmp.b))
                        kb.barrier()
                    kb.op("pool", lambda e: e.memset(maskT.t[:], 0.0), writes=[maskT.b])
                    acc = kb.sb(es1, "cd_acc", [128, 32 * 128], F32)
                    work = kb.sb(es1, "cd_work", [128, 32 * 128], F32)
                    mkf = kb.sb(es1, "cd_mkf", [128, 32 * 128], F32)
                    qi = [kb.sb(es1, "cd_qi%d" % i, [128, 16, 128], bf) for i in range(2)]
                    rl = [kb.sb(es1, "cd_rl%d" % i, [128, 512], F32) for i in range(3)]
                    m8 = kb.sb(es1, "cd_m8", [128, 8], F32)
                    thr = kb.sb(es1, "cd_thr", [128, 1], F32)
                    kk = kidx_k.t[:].rearrange("p s t -> p (s t)")
                    nr = 0
                    for ib in range(4):
                        il = hf * 4 + ib
                        pos = 24 + il
                        S = (pos + 1) * 128
                        qt = qi[ib % 2]
                        kb.dma("sp", qt.t[:], qi_d.t[:, :, il * 128:(il + 1) * 128].rearrange("f p t -> p f t"), reads=[qi_d.b], writes=[qt.b])
                        for hi in range(32):
                            fc, po = hi // 2, (hi % 2) * 64
                            for sc_ in range((S + 511) // 512):
                                n = min(512, S - sc_ * 512)
                                ps = nextps()
                                kb.op("pe", lambda e: e.matmul(ps.t[:, :n], qt.t[po:po + 64, fc, :], kk[po:po + 64, sc_ * 512:sc_ * 512 + n], start=True, stop=True),
                                      reads=[qt.b, kidx_k.b], writes=[ps.b], self_sync=False)
                                r_ = rl[nr % 3]
                                nr += 1
                                kb.op("act", lambda e: e.activation(out=r_.t[:, :n], in_=ps.t[:, :n], func=AF.Relu), reads=[ps.b], writes=[r_.b])
                                a_ap = acc.t[:, sc_ * 512:sc_ * 512 + n]
                                wsc = w_tm.t[:, il, hi:hi + 1]
                                if hi == 0:
                                    kb.op("dve", lambda e: e.tensor_scalar(out=a_ap, in0=r_.t[:, :n], scalar1=wsc, scalar2=None, op0=ALU.mult),
                                          reads=[r_.b, w_tm.b], writes=[acc.b])
                                else:
                                    kb.op("dve", lambda e: e.scalar_tensor_tensor(out=a_ap, in0=r_.t[:, :n], scalar=wsc, in1=a_ap, op0=ALU.mult, op1=ALU.add),
                                          reads=[r_.b, w_tm.b, acc.b], writes=[acc.b])
                        for sl in range(3):
                            kb.op("dve", lambda e: e.tensor_scalar(out=acc.t[:, sl * TOK:(sl + 1) * TOK], in0=acc.t[:, sl * TOK:(sl + 1) * TOK],
                                                                   scalar1=P["negslot"].t[:, sl:sl + 1], scalar2=None, op0=ALU.add),
                                  reads=[acc.b, P["negslot"].b], writes=[acc.b])
                        kb.op("dve", lambda e: e.tensor_tensor(out=acc.t[:, pos * 128:S], in0=acc.t[:, pos * 128:S], in1=P["negtri"].t[:], op=ALU.add),
                              reads=[acc.b, P["negtri"].b], writes=[acc.b])
                        kb.op("pool", lambda e: e.tensor_copy(work.t[:, :S], acc.t[:, :S]), reads=[acc.b], writes=[work.b])
                        for rd in range(32):
                            kb.op("dve", lambda e: e.max(out=m8.t[:], in_=work.t[:, :S]), reads=[work.b], writes=[m8.b])
                            if rd < 31:
                                kb.op("dve", lambda e: e.match_replace(out=work.t[:, :S], in_to_replace=m8.t[:], in_values=work.t[:, :S], imm_value=NEG),
                                      reads=[m8.b, work.b], writes=[work.b])
                        kb.op("dve", lambda e: e.tensor_reduce(out=thr.t[:], in_=m8.t[:], axis=AX.X, op=ALU.min), reads=[m8.b], writes=[thr.b])
                        kb.op("dve", lambda e: e.tensor_scalar(out=mkf.t[:, :S], in0=acc.t[:, :S], scalar1=thr.t[:, 0:1], scalar2=None, op0=ALU.is_ge),
                              reads=[acc.b, thr.b], writes=[mkf.b])
                        for sl in range(3):
                            kb.op("pool", lambda e: e.tensor_scalar(out=mkf.t[:, sl * TOK:(sl + 1) * TOK], in0=mkf.t[:, sl * TOK:(sl + 1) * TOK],
                                                                    scalar1=P["valid01"].t[:, sl:sl + 1], scalar2=None, op0=ALU.mult),
                                  reads=[mkf.b, P["valid01"].b], writes=[mkf.b])
                        kb.op("pool", lambda e: e.tensor_tensor(out=mkf.t[:, pos * 128:S], in0=mkf.t[:, pos * 128:S], in1=P["tril"].t[:], op=ALU.mult),
                              reads=[mkf.b, P["tril"].b], writes=[mkf.b])
                        for j4 in range((pos + 4) // 4):
                            ps = nextps()
                            nj = min(4, pos + 1 - j4 * 4)

                            def mmt(e):
                                ins = None
                                for jj in range(nj):
                                    js = j4 * 4 + jj
                                    ins = e.transpose(ps.t[:, jj * 128:(jj + 1) * 128], mkf.t[:, js * 128:(js + 1) * 128], C.ident.t[:])
                                return ins
                            kb.op("pe", mmt, reads=[mkf.b, C.ident.b], writes=[ps.b], self_sync=False)
                            o = maskT.t[:, j4 * 4:j4 * 4 + nj, ib * 128:(ib + 1) * 128]
                            i_ = ps.t[:, :nj * 128].rearrange("p (j t) -> p j t", j=nj)
                            kb.op("act", lambda e: e.copy(o, i_), reads=[ps.b], writes=[maskT.b])
                    kb.barrier()
                yD = kb.sb(esK, "cd_yD", [128, 16, NH], bf)
                yDb = [Buf() for _ in range(16)]
                with ExitStack() as es1:
                    ckvT_k = kb.sb(es1, "cd_ckvTk", [128, 4, 4, TOK], bf)
                    ckvm_k = kb.sb(es1, "cd_ckvmk", [128, 4, 8, 512], bf)
                    with ExitStack() as es2:
                        tmp = kb.sb(es2, "cd_kvtmp", [128, 4, 4, TOK], bf)
                        for q in range(4):
                            kb.dma("sp", tmp.t[:, q], ckvT_a.t[q * 512:(q + 1) * 512, :].rearrange("(rc p) t -> p rc t", p=128),
                                   reads=[ckvT_a.b], writes=[tmp.b])
                        kb.op("pool", lambda e: e.tensor_copy(ckvT_k.t[:, :, 3, :], ckvT_own.t[:]), reads=ckvTob, writes=[ckvT_k.b])
                        sel_combine(lambda sl: (ckvT_k.t[:, :, sl, :], ckvT_k.b), lambda q: (tmp.t[:, q], tmp.b))
                        kb.barrier()
                    with ExitStack() as es2:
                        tmp = kb.sb(es2, "cd_kmtmp", [128, 4, 8, 512], bf)
                        for q in range(4):
                            kb.dma("sp", tmp.t[:, q], ckvm_a.t[q * TOK:(q + 1) * TOK, :].rearrange("(c p) r -> p c r", p=128),
                                   reads=[ckvm_a.b], writes=[tmp.b])
                        kb.dma("sp", ckvm_k.t[:, 3], ckvm_s.t[:, :].rearrange("(c p) r -> p c r", p=128), reads=[ckvm_s.b], writes=[ckvm_k.b])
                        sel_combine(lambda sl: (ckvm_k.t[:, sl], ckvm_k.b), lambda q: (tmp.t[:, q], tmp.b))
                        kb.barrier()
                    qh = [kb.sb(es1, "cd_qh%d" % i, [128, NH], bf) for i in range(2)]
                    wuk = [kb.sb(es1, "cd_wuk%d" % i, [128, 512], bf) for i in range(2)]
                    wuv = [kb.sb(es1, "cd_wuv%d" % i, [128, 4, 128], bf) for i in range(2)]
                    btt = [kb.sb(es1, "cd_btt%d" % i, [128, 5, NH], bf) for i in range(1)]
                    qlat = [kb.sb(es1, "cd_qlat%d" % i, [128, 4, NH], bf) for i in range(2)]
                    Eb = [kb.sb(es1, "cd_E%d" % i, [128, NH], bf) for i in range(3)]
                    pmb = [kb.sb(es1, "cd_pm%d" % i, [128, NH], bf) for i in range(3)]
                    rden = kb.sb(es1, "cd_rden", [128, NH], F32)
                    olat = kb.sb(es1, "cd_olat", [128, 4, NH], bf)
                    accb = [C.ps[i] for i in range(4)]
                    denb = C.ps[4]
                    lgb = [C.ps[5], C.ps[6]]
                    xb_ = C.ps[7]
                    ne = 0
                    for h in range(16):
                        kb.bg(2)
                        q_, uk_, uv_, bt_, ql_ = qh[h % 2], wuk[h % 2], wuv[h % 2], btt[0], qlat[h % 2]
                        kb.dma("sp", q_.t[:], q_d.t[h][:, n0:n0 + NH], reads=[q_d.b], writes=[q_.b])
                        kb.dma("sp", uk_.t[:], w_uk.t[h * 128:(h + 1) * 128, :], reads=[w_uk.b], writes=[uk_.b])
                        kb.dma("sp", uv_.t[:], w_uv.t[h * 512:(h + 1) * 512, :].rearrange("(rc p) d -> p rc d", p=128), reads=[w_uv.b], writes=[uv_.b])
                        kb.dma("sp", bt_.t[:], bass.AP(RN_t, h * RNK + 1, [[1, 128], [128, 5], [1, NH]]), reads=[RN_d.b], writes=[bt_.b])
                        for rc in range(4):
                            ps = lgb[rc % 2]
                            kb.op("pe", lambda e: e.matmul(ps.t[:, :NH], uk_.t[:, rc * 128:(rc + 1) * 128], q_.t[:], start=True, stop=True),
                                  reads=[uk_.b, q_.b], writes=[ps.b], self_sync=False)
                            kb.op("act", lambda e: e.copy(ql_.t[:, rc, :], ps.t[:, :NH]), reads=[ps.b], writes=[ql_.b])
                        for js in range(JS):
                            sl, ch = js // 8, js % 8
                            dj = js - b0
                            near = dj >= -1
                            ps = lgb[js % 2]

                            def mml(e):
                                ins = None
                                for rc in range(4):
                                    ins = e.matmul(ps.t[:, :NH], ckvT_k.t[:, rc, sl, ch * 128:(ch + 1) * 128], ql_.t[:, rc, :],
                                                   start=(rc == 0), stop=(rc == 3 and not near))
                                if near:
                                    ins = e.matmul(ps.t[:, :NH], Jb.t[:], bt_.t[:, 3 - dj, :], start=False, stop=True)
                                return ins
                            kb.op("pe", mml, reads=[ckvT_k.b, ql_.b, Jb.b, bt_.b], writes=[ps.b], self_sync=False)
                            E_ = Eb[ne % 3]
                            pm_ = pmb[ne % 3]
                            ne += 1
                            if near:
                                kb.op("act", lambda e: e.activation(out=E_.t[:], in_=ps.t[:, :NH], func=AF.Exp, scale=SC), reads=[ps.b], writes=[E_.b])
                            else:
                                kb.op("act", lambda e: e.activation(out=E_.t[:], in_=ps.t[:, :NH], func=AF.Exp, scale=SC, bias=P["rb31"].t[:, h:h + 1]),
                                      reads=[ps.b, P["rb31"].b], writes=[E_.b])
                            kb.op("dve" if ne % 2 else "pool", lambda e: e.tensor_tensor(out=pm_.t[:], in0=E_.t[:], in1=maskT.t[:, js, :], op=ALU.mult),
                                  reads=[E_.b, maskT.b], writes=[pm_.b])

                            def mmv(e):
                                ins = None
                                for rc in range(4):
                                    ins = e.matmul(accb[rc].t[:, :NH], ckvm_k.t[:, sl, ch, rc * 128:(rc + 1) * 128], pm_.t[:],
                                                   start=(js == 0), stop=(js == JS - 1))
                                ins = e.matmul(denb.t[:, :NH], onesb.t[:], pm_.t[:], start=(js == 0), stop=(js == JS - 1))
                                return ins
                            kb.op("pe", mmv, reads=[ckvm_k.b, pm_.b, onesb.b], writes=[a.b for a in accb] + [denb.b], self_sync=False)
                        kb.op("dve", lambda e: e.reciprocal(out=rden.t[:], in_=denb.t[:, :NH]), reads=[denb.b], writes=[rden.b])
                        for rc in range(4):
                            kb.op("dve", lambda e: e.tensor_tensor(out=olat.t[:, rc, :], in0=accb[rc].t[:, :NH], in1=rden.t[:], op=ALU.mult),
                                  reads=[accb[rc].b, rden.b], writes=[olat.b])

                        def mmo(e):
                            ins = None
                            for rc in range(4):
                                ins = e.matmul(xb_.t[:, :NH], uv_.t[:, rc, :], olat.t[:, rc, :], start=(rc == 0), stop=(rc == 3))
                            return ins
                        kb.op("pe", mmo, reads=[uv_.b, olat.b], writes=[xb_.b], self_sync=False)
                        kb.op("act", lambda e: e.copy(yD.t[:, h, :], xb_.t[:, :NH]), reads=[xb_.b], writes=[yDb[h]])
                    kb.barrier()
                out_proj(kb, C, w_out, lambda fc: (yD.t[:, fc, :], yDb[fc]), n0, NH, "cd_od", nk=16, krow0=16)
            kb.barrier()
        kb.barrier()

SMALL = {
    "ident": ([128, 128], None),
    "g_ff": ([128, 4 * NKC], None),
    "g_final": ([128, NKC], None),
    "g_mix": ([128, 2 * NKC], ("ab", "cd")),
    "sel": ([128, 4], ("ab", "cd")),
    "lnag": ([128, 16], ("ab",)), "lnab": ([128, 16], ("ab",)), "lnbg": ([128, 16], ("ab",)), "lnbb": ([128, 16], ("ab",)),
    "convw": ([128, 16, KW], ("ab",)), "convb": ([128, 16], ("ab",)),
    "wspT": ([128, 8, 128], ("ab",)), "tri": ([128, 128], ("ab",)), "bspbc": ([128, 8, 128], ("ab",)),
    "poolsc": ([128, 16], ("cd",)), "gcq": ([128, 8], ("cd",)), "gckv": ([128, 4], ("cd",)), "selr": ([128, 12], ("cd",)),
    "negslot": ([128, 3], ("cd",)), "valid01": ([128, 3], ("cd",)), "negtri": ([128, 128], ("cd",)), "tril": ([128, 128], ("cd",)),
    "Jf": ([128, 128], ("cd",)), "rb": ([128, 16], ("cd",)), "OHk": ([128, RNK], ("cd",)), "rb31": ([128, 16], ("cd",)),
}


def small_needed(stages):
    return [k for k, (shp, st) in SMALL.items() if st is None or any(x in stages for x in st)]


def build_program(stages, dump=False):
    nc = bass.Bass("TRN2", target_bir_lowering=False)
    es = ExitStack()
    with es:
        kb = KB(nc, es)
        C = Ctx()
        x_in = nc.dram_tensor("x", [TOK, D], F32, kind="ExternalInput").ap()
        out_ap = nc.dram_tensor("out", [TOK, D], F32, kind="ExternalOutput").ap()
        xT_d = nc.dram_tensor("xT_res", [NKC, 128, TOK], F32).ap()
        C.xTb = [[Buf(), Buf()] for _ in range(NKC)]
        C.outb = Buf()
        C.xT = xT_d
        P = {}
        sm_in = {}
        for name in small_needed(stages):
            shp = SMALL[name][0]
            flat = [128, int(np.prod(shp[1:]))]
            sm_in[name] = nc.dram_tensor(name, flat, F32, kind="ExternalInput").ap()
            P[name] = kb.sb(es, name + "_sb", shp, F32)

        W = {}
        seq = []
        for st in stages:
            if st.startswith("ffn"):
                l, j = int(st[3]), int(st[4])
                wi = make_weight(kb, nc, "w_ff_in_%d%d" % (l, j), 2 * DFF, D)
                wo = make_weight(kb, nc, "w_ff_out_%d%d" % (l, j), DFF, D)
                W[st] = (wi, wo)
                f0 = 0
                for G in FFN_GROUPS:
                    seq += wi.pieces_for_rows(2 * f0 * 128, 2 * (f0 + G) * 128)
                    seq += wo.pieces_for_rows(f0 * 128, (f0 + G) * 128)
                    f0 += G
            elif st == "ab":
                W[st] = (make_weight(kb, nc, "w_in_ab", D, 8192), make_weight(kb, nc, "w_out_ab", D, D))
                for w in W[st]:
                    seq += w.all_pieces()
            elif st == "cd":
                W[st] = (make_weight(kb, nc, "w_in_cd", D, 3680), make_weight(kb, nc, "w_out_cd", D, D),
                         make_weight(kb, nc, "w_pool", 2048, 512), make_weight(kb, nc, "w_uq", 1024, 2048),
                         make_weight(kb, nc, "w_qidx", 1024, 2048), make_weight(kb, nc, "w_uk", 2048, 512),
                         make_weight(kb, nc, "w_uv", 8192, 128))
                P["invcnt_d"] = T(nc.dram_tensor("invcnt", [128, 4 * TOK], F32, kind="ExternalInput").ap())
                for w in (W[st][0], W[st][2], W[st][3], W[st][4], W[st][5], W[st][6], W[st][1]):
                    seq += w.all_pieces()
        enqueue_pieces(kb, seq)

        C.ps = [T(es.enter_context(nc.psum_tensor("ps%d" % i, [128, 512], F32)), Buf()) for i in range(8)]
        C.ident = P["ident"]
        C.ones = kb.sb(es, "ones_sb", [128, 128], F32)
        C.gff = P["g_ff"]
        C.gfin = P["g_final"]
        if "sel" in P:
            C.sel = P["sel"]
        if "g_mix" in P:
            P["gmix0"] = T(P["g_mix"].t[:, 0:NKC], P["g_mix"].b)
            P["gmix1"] = T(P["g_mix"].t[:, NKC:2 * NKC], P["g_mix"].b)

        block = es.enter_context(nc.Block())

        @block.gpsimd
        def _(g):
            for name, src in sm_in.items():
                t = P[name]
                shp = SMALL[name][0]
                if len(shp) == 2:
                    dst = t.t[:]
                else:
                    dst = t.t[:].rearrange("p a b -> p (a b)")
                kb.dma("sp", dst, src[:, :], writes=[t.b])
            kb.op("dve", lambda e: e.memset(C.ones.t[:], 1.0), writes=[C.ones.b])
            kb.bg(12)
            load_x(kb, C, x_in)
            for si, st in enumerate(stages):
                if st.startswith("ffn"):
                    l, j = int(st[3]), int(st[4])
                    gi = l * 2 + j
                    gT = T(C.gff.t[:, gi * NKC:(gi + 1) * NKC], C.gff.b)
                    ffn_stage(kb, C, gT, W[st][0], W[st][1], bg_per_group=2)
                elif st == "ab":
                    for w in W[st]:
                        w.need_all(kb)
                    mixer_ab(kb, C, nc, P, W[st][0], W[st][1])
                elif st == "cd":
                    for w in W[st]:
                        w.need_all(kb)
                    mixer_cd(kb, C, nc, P, W[st])
            kb.bg(100000)
            final_norm_store(kb, C, C.gfin, out_ap, do_norm=not dump)
            kb.barrier()
    return nc


ALL_STAGES = ["ffn00", "ab", "ffn01", "ffn10", "cd", "ffn11"]


def lay128(v):
    v = np.asarray(v, np.float32)
    return np.ascontiguousarray(v.reshape(-1, 128).T)


def small_values(inputs, c):
    f = np.float32
    v = {}
    v["ident"] = np.eye(128, dtype=f)
    v["g_ff"] = np.concatenate([lay128(inputs["g_ff"][l, j]) for l in range(2) for j in range(2)], axis=1)
    v["g_final"] = lay128(inputs["g_final"])
    v["g_mix"] = np.concatenate([lay128(inputs["g_mix"][0]), lay128(inputs["g_mix"][1])], axis=1)
    sel = np.zeros((128, 4), f)
    if c % 4 > 0:
        sel[:, c % 4 - 1] = 1.0
    v["sel"] = sel
    v["lnag"] = lay128(inputs["ln_a_g"][0])
    v["lnab"] = lay128(inputs["ln_a_b"][0])
    v["lnbg"] = lay128(inputs["ln_b_g"][0])
    v["lnbb"] = lay128(inputs["ln_b_b"][0])
    cw = np.asarray(inputs["conv_w"][0], f)
    v["convw"] = cw.T.reshape(16, 128, KW).transpose(1, 0, 2)
    v["convb"] = lay128(inputs["conv_b"][0])
    v["wspT"] = np.asarray(inputs["w_sp"][0], f).transpose(2, 0, 1)
    jj = np.arange(128)
    v["tri"] = (jj[:, None] <= jj[None, :]).astype(f)
    v["bspbc"] = np.broadcast_to(np.asarray(inputs["b_sp"][0], f)[None], (128, 8, 128))
    cq = c % 4
    v["poolsc"] = lay128(inputs["pool_scale"][0])
    v["gcq"] = lay128(inputs["g_cq"][0])
    v["gckv"] = lay128(inputs["g_ckv"][0])
    selr = np.zeros((128, 3, 4), f)
    neg = np.zeros((128, 3), f)
    val = np.zeros((128, 3), f)
    for r in (1, 2, 3):
        if cq - r >= 0:
            selr[:, r - 1, cq - r] = 1.0
    for sl in range(3):
        r = 3 - sl
        ok = cq - r >= 0
        neg[:, sl] = 0.0 if ok else NEG
        val[:, sl] = 1.0 if ok else 0.0
    v["selr"] = selr
    v["negslot"] = neg
    v["valid01"] = val
    tt = np.arange(128)
    v["negtri"] = np.where(tt[None, :] <= tt[:, None], 0.0, NEG).astype(f)
    v["tril"] = (tt[None, :] <= tt[:, None]).astype(f)
    v["Jf"] = np.eye(128, dtype=f)[::-1].copy()
    rb = np.zeros((128, 16), f)
    rb[:32] = np.asarray(inputs["rel_bias"], f)
    v["rb"] = rb
    n = np.arange(RNK) - RNPAD
    nf = np.maximum(n, 1).astype(f)
    large = 16 + (np.log(nf / f(16)) / f(np.log(128 / 16)) * f(16)).astype(np.int32)
    large = np.minimum(large, 31)
    bucket = np.where(n < 16, np.maximum(n, 0), large)
    oh = np.zeros((128, RNK), f)
    for k in range(RNK):
        if n[k] >= 0:
            oh[bucket[k], k] = 1.0
    v["OHk"] = oh
    v["rb31"] = np.broadcast_to(np.asarray(inputs["rel_bias"], f)[31][None], (128, 16))
    pos = cq * TOK + np.arange(TOK)
    inv = np.stack([1.0 / np.minimum(pos + 1, w) for w in (2, 4, 8, 16)], 0).astype(f)
    v["invcnt"] = np.broadcast_to(inv.reshape(1, 4 * TOK), (128, 4 * TOK))
    return v


def tile_w_in(w):
    w = np.asarray(w, np.float32)
    return np.ascontiguousarray(w.reshape(NKC, 128, 2, NFC, 128).transpose(3, 2, 1, 0, 4)).reshape(2 * DFF, D)


def make_in_maps(inputs, stages):
    x = np.asarray(inputs["x"], np.float32).reshape(8 * TOK, D)
    names = small_needed(stages)
    maps = []
    for c in range(8):
        m = {"x": x[c * TOK:(c + 1) * TOK]}
        sv = small_values(inputs, c)
        for n in names:
            m[n] = np.ascontiguousarray(np.asarray(sv[n], np.float32).reshape(128, -1))
        if "cd" in stages:
            m["invcnt"] = np.ascontiguousarray(sv["invcnt"], dtype=np.float32)
        maps.append(m)

    def put(name, w):
        w = np.asarray(w, np.float32)
        for c in range(8):
            maps[c][name] = shard_weight(w, c)

    for st in stages:
        if st.startswith("ffn"):
            l, j = int(st[3]), int(st[4])
            put("w_ff_in_%d%d" % (l, j), tile_w_in(inputs["w_ff_in"][l, j]))
            put("w_ff_out_%d%d" % (l, j), inputs["w_ff_out"][l, j])
        elif st == "ab":
            put("w_in_ab", inputs["w_in_ab"][0])
            put("w_out_ab", inputs["w_out_ab"][0])
        elif st == "cd":
            put("w_in_cd", inputs["w_in_cd"][0])
            put("w_out_cd", inputs["w_out_cd"][0])
            put("w_pool", np.asarray(inputs["w_pool"][0]).reshape(2048, 512))
            put("w_uq", inputs["w_uq"][0])
            put("w_qidx", inputs["w_qidx"][0])
            put("w_uk", np.asarray(inputs["w_uk"][0]).reshape(2048, 512))
            put("w_uv", np.asarray(inputs["w_uv"][0]).reshape(8192, 128))
    return maps


def run(inputs, stages, dump=False):
    nc = build_program(stages, dump=dump)
    maps = make_in_maps(inputs, stages)
    res = run_bass_kernel_spmd(nc, maps, core_ids=list(range(8)))
    out = np.concatenate([res.results[c]["out"] for c in range(8)], axis=0)
    return out.reshape(2, 4096, D)


def kernel(**inputs):
    return run(inputs, ALL_STAGES, dump=False)
```
